# Optimizing a Trainium2 kernel written in Bass

```python
import jax, jax.numpy as jnp
from jax import lax
import numpy as np

D_MODEL = 1024
BATCH = 2
SEQ = 8192
DEPTH = 1

GRID_W = 64
CTX_LEN = 256
ATT_HEADS = 8
ATT_KV_HEADS = 2
ATT_HEAD_DIM = 64
ATT_GROUP = ATT_HEADS // ATT_KV_HEADS
ATT_WIDTH = ATT_HEADS * ATT_HEAD_DIM
ATT_KV_WIDTH = ATT_KV_HEADS * ATT_HEAD_DIM
RET_HEADS = 4
RET_HEAD_DIM = 128
RET_WIDTH = RET_HEADS * RET_HEAD_DIM
MIX_WIDTH = ATT_WIDTH + RET_WIDTH
IN_SIZES = (ATT_WIDTH, ATT_KV_WIDTH, ATT_KV_WIDTH, ATT_WIDTH, RET_WIDTH, RET_WIDTH, RET_WIDTH, RET_WIDTH)
IN_WIDTH = sum(IN_SIZES)
Q_BLOCK = 128
RET_CHUNK = 128
ROPE_THETA = 10000.0
NORM_EPS = 1e-6

kernel_name = "hymba_style_gqa_retention_prefix_ctx"


def rms_norm(x, w):
    xf = x.astype(jnp.float32)
    y = xf * lax.rsqrt(jnp.mean(xf * xf, axis=-1, keepdims=True) + NORM_EPS)
    return (y * w.astype(jnp.float32)).astype(x.dtype)


def split_proj(p):
    offsets = [int(o) for o in np.cumsum(IN_SIZES)[:-1]]
    return jnp.split(p, offsets, axis=-1)


def axial_rope_tables(n_rows, head_dim):
    rows, cols = jnp.meshgrid(jnp.arange(n_rows, dtype=jnp.float32),
                              jnp.arange(GRID_W, dtype=jnp.float32), indexing="ij")
    rows = rows.reshape(-1)
    cols = cols.reshape(-1)
    n_axis = head_dim // 4
    inv_freq = ROPE_THETA ** (-jnp.arange(n_axis, dtype=jnp.float32) / n_axis)
    ang = jnp.concatenate([rows[:, None] * inv_freq, cols[:, None] * inv_freq], axis=-1)
    return jnp.cos(ang), jnp.sin(ang)


def apply_rope(x, cos, sin):
    half = x.shape[-1] // 2
    xf = x.astype(jnp.float32)
    x1, x2 = xf[..., :half], xf[..., half:]
    c = cos[None, :, None, :]
    s = sin[None, :, None, :]
    return jnp.concatenate([x1 * c - x2 * s, x1 * s + x2 * c], axis=-1).astype(x.dtype)


def attention_heads(qa, ka, va, q_norm_w, k_norm_w):
    b, t, _ = qa.shape
    q = rms_norm(qa.reshape(b, t, ATT_HEADS, ATT_HEAD_DIM), q_norm_w)
    k = rms_norm(ka.reshape(b, t, ATT_KV_HEADS, ATT_HEAD_DIM), k_norm_w)
    v = va.reshape(b, t, ATT_KV_HEADS, ATT_HEAD_DIM)
    return q, k, v


def latent_attention(q_lat, k_lat, v_lat, k_ctx, v_ctx):
    b, t, _, _ = q_lat.shape
    scale = ATT_HEAD_DIM ** -0.5
    k_all = jnp.concatenate([k_ctx, k_lat], axis=1)
    v_all = jnp.concatenate([v_ctx, v_lat], axis=1)
    nb = t // Q_BLOCK
    qb = q_lat.reshape(b, nb, Q_BLOCK, ATT_KV_HEADS, ATT_GROUP, ATT_HEAD_DIM)
    qb = qb.transpose(1, 0, 3, 4, 2, 5)

    def one_block(q_blk):
        s = jnp.einsum("bkgqd,bskd->bkgqs", q_blk, k_all,
                       preferred_element_type=jnp.float32) * scale
        p = jax.nn.softmax(s, axis=-1).astype(v_all.dtype)
        return jnp.einsum("bkgqs,bskd->bkgqd", p, v_all)

    o = lax.map(one_block, qb)
    return o.transpose(1, 0, 4, 2, 3, 5).reshape(b, t, ATT_WIDTH)


def context_attention(q, k, v):
    b, l, _, _ = q.shape
    qg = q.reshape(b, l, ATT_KV_HEADS, ATT_GROUP, ATT_HEAD_DIM)
    s = jnp.einsum("bqkgd,bskd->bkgqs", qg, k,
                   preferred_element_type=jnp.float32) * (ATT_HEAD_DIM ** -0.5)
    p = jax.nn.softmax(s, axis=-1).astype(v.dtype)
    o = jnp.einsum("bkgqs,bskd->bqkgd", p, v)
    return o.reshape(b, l, ATT_WIDTH)


def retention_heads(qr, kr, vr, cos=None, sin=None):
    b, t, _ = qr.shape
    q = qr.reshape(b, t, RET_HEADS, RET_HEAD_DIM)
    k = kr.reshape(b, t, RET_HEADS, RET_HEAD_DIM)
    v = vr.reshape(b, t, RET_HEADS, RET_HEAD_DIM)
    if cos is not None:
        q = apply_rope(q, cos, sin)
        k = apply_rope(k, cos, sin)
    k = k * (RET_HEAD_DIM ** -0.5)
    to_bhtd = lambda a: a.transpose(0, 2, 1, 3).astype(jnp.float32)
    return to_bhtd(q), to_bhtd(k), to_bhtd(v)


def retention_chunked(q, k, v, log_gamma, state0):
    b, h, t, dk = q.shape
    dv = v.shape[-1]
    n = t // RET_CHUNK
    qc = q.reshape(b, h, n, RET_CHUNK, dk)
    kc = k.reshape(b, h, n, RET_CHUNK, dk)
    vc = v.reshape(b, h, n, RET_CHUNK, dv)
    idx = jnp.arange(RET_CHUNK, dtype=jnp.float32)
    diff = idx[:, None] - idx[None, :]
    lg = log_gamma[:, None, None]
    decay_in = jnp.where(diff >= 0, jnp.exp(lg * jnp.maximum(diff, 0.0)), 0.0)
    q_decay = jnp.exp(log_gamma[:, None] * (idx + 1.0))
    k_decay = jnp.exp(log_gamma[:, None] * (RET_CHUNK - 1.0 - idx))
    chunk_decay = jnp.exp(log_gamma * RET_CHUNK)

    scores = jnp.einsum("bhnid,bhnjd->bhnij", qc, kc) * decay_in[None, :, None]
    o_inner = jnp.einsum("bhnij,bhnjv->bhniv", scores, vc)
    u = jnp.einsum("bhnjd,bhnjv->nbhdv", kc * k_decay[None, :, None, :, None], vc)

    def step(s, u_n):
        s_new = chunk_decay[None, :, None, None] * s + u_n
        return s_new, s

    s_final, s_prev = lax.scan(step, state0, u)
    o_cross = jnp.einsum("bhnid,nbhdv->bhniv", qc * q_decay[None, :, None, :, None], s_prev)
    return (o_inner + o_cross).reshape(b, h, t, dv), s_final


def retention_bidir(q, k, v, lg_fwd, lg_bwd, s_fwd0, s_bwd0):
    o_f, s_f = retention_chunked(q, k, v, lg_fwd, s_fwd0)
    flip = lambda a: jnp.flip(a, axis=2)
    o_b, s_b = retention_chunked(flip(q), flip(k), flip(v), lg_bwd, s_bwd0)
    return o_f + flip(o_b), s_f, s_b


def head_group_norm(o, w, dtype):
    mu = jnp.mean(o, axis=-1, keepdims=True)
    var = jnp.mean(jnp.square(o - mu), axis=-1, keepdims=True)
    y = (o - mu) * lax.rsqrt(var + NORM_EPS)
    b, h, t, dv = o.shape
    y = y.transpose(0, 2, 1, 3).reshape(b, t, h * dv) * w.astype(jnp.float32)
    return y.astype(dtype)


def setup_inputs(seed: int = 0) -> dict:
    key = jax.random.key(seed)
    ks = jax.random.split(key, 16)
    f32 = jnp.float32
    eps = 2.0 ** (-5.0 - np.arange(RET_HEADS, dtype=np.float32))
    decay_logit = jnp.asarray(np.log((1.0 - eps) / eps), dtype=f32)
    return {
        "x": jax.random.normal(ks[0], (BATCH, SEQ, D_MODEL), f32),
        "c": jax.random.normal(ks[1], (BATCH, D_MODEL), f32),
        "ctx": jax.random.normal(ks[2], (BATCH, CTX_LEN, D_MODEL), f32),
        "c_ctx": jax.random.normal(ks[3], (D_MODEL,), f32),
        "norm_w": 1.0 + 0.02 * jax.random.normal(ks[4], (DEPTH, D_MODEL), f32),
        "w_mod": 0.5 * D_MODEL ** -0.5 * jax.random.normal(ks[5], (DEPTH, D_MODEL, 3 * D_MODEL), f32),
        "b_mod": 0.02 * jax.random.normal(ks[6], (DEPTH, 3 * D_MODEL), f32),
        "w_in": D_MODEL ** -0.5 * jax.random.normal(ks[7], (DEPTH, D_MODEL, IN_WIDTH), f32),
        "q_norm_w": 1.0 + 0.02 * jax.random.normal(ks[8], (DEPTH, ATT_HEAD_DIM), f32),
        "k_norm_w": 1.0 + 0.02 * jax.random.normal(ks[9], (DEPTH, ATT_HEAD_DIM), f32),
        "ret_decay_fwd": decay_logit + 0.05 * jax.random.normal(ks[10], (DEPTH, RET_HEADS), f32),
        "ret_decay_bwd": decay_logit + 0.05 * jax.random.normal(ks[11], (DEPTH, RET_HEADS), f32),
        "ret_gn_w": 1.0 + 0.02 * jax.random.normal(ks[12], (DEPTH, RET_WIDTH), f32),
        "w_out": MIX_WIDTH ** -0.5 * jax.random.normal(ks[13], (DEPTH, MIX_WIDTH, D_MODEL), f32),
        "final_norm_w": 1.0 + 0.02 * jax.random.normal(ks[14], (D_MODEL,), f32),
    }


def reference(x, c, ctx, c_ctx, norm_w, w_mod, b_mod, w_in, q_norm_w, k_norm_w,
              ret_decay_fwd, ret_decay_bwd, ret_gn_w, w_out, final_norm_w):
    b, seq, _ = x.shape
    n_rows = seq // GRID_W
    cos_a, sin_a = axial_rope_tables(n_rows, ATT_HEAD_DIM)
    cos_r, sin_r = axial_rope_tables(n_rows, RET_HEAD_DIM)
    zero_state = jnp.zeros((b, RET_HEADS, RET_HEAD_DIM, RET_HEAD_DIM), jnp.float32)

    for layer in range(DEPTH):
        shift, scale, gate = jnp.split(jax.nn.silu(c) @ w_mod[layer] + b_mod[layer], 3, axis=-1)
        shift_c, scale_c, gate_c = jnp.split(jax.nn.silu(c_ctx) @ w_mod[layer] + b_mod[layer], 3, axis=-1)
        h = rms_norm(x, norm_w[layer]) * (1.0 + scale[:, None, :]) + shift[:, None, :]
        hc = rms_norm(ctx, norm_w[layer]) * (1.0 + scale_c) + shift_c

        qa, ka, va, ga, qr, kr, vr, gr = split_proj(h @ w_in[layer])
        qa_c, ka_c, va_c, ga_c, qr_c, kr_c, vr_c, gr_c = split_proj(hc @ w_in[layer])

        q_l, k_l, v_l = attention_heads(qa, ka, va, q_norm_w[layer], k_norm_w[layer])
        q_l = apply_rope(q_l, cos_a, sin_a)
        k_l = apply_rope(k_l, cos_a, sin_a)
        q_c, k_c, v_c = attention_heads(qa_c, ka_c, va_c, q_norm_w[layer], k_norm_w[layer])
        att_lat = latent_attention(q_l, k_l, v_l, k_c, v_c)

        lg_f = jax.nn.log_sigmoid(ret_decay_fwd[layer].astype(jnp.float32))
        lg_b = jax.nn.log_sigmoid(ret_decay_bwd[layer].astype(jnp.float32))
        rq_c, rk_c, rv_c = retention_heads(qr_c, kr_c, vr_c)
        ret_ctx, s_fwd_ctx, s_bwd_ctx = retention_bidir(rq_c, rk_c, rv_c, lg_f, lg_b, zero_state, zero_state)
        rq_l, rk_l, rv_l = retention_heads(qr, kr, vr, cos_r, sin_r)
        ret_lat, _, _ = retention_bidir(rq_l, rk_l, rv_l, lg_f, lg_b, s_fwd_ctx, s_bwd_ctx)
        ret_lat = head_group_norm(ret_lat, ret_gn_w[layer], x.dtype)

        mixed = jnp.concatenate([att_lat * jax.nn.silu(ga), ret_lat * jax.nn.silu(gr)], axis=-1)
        x = x + gate[:, None, :] * (mixed @ w_out[layer])

        if layer < DEPTH - 1:
            att_ctx = context_attention(q_c, k_c, v_c)
            ret_ctx_n = head_group_norm(ret_ctx, ret_gn_w[layer], ctx.dtype)
            mixed_c = jnp.concatenate([att_ctx * jax.nn.silu(ga_c), ret_ctx_n * jax.nn.silu(gr_c)], axis=-1)
            ctx = ctx + gate_c * (mixed_c @ w_out[layer])

    return rms_norm(x, final_norm_w)
```

```python
import os
from contextlib import ExitStack

import numpy as np
import concourse.bass as bass
import concourse.mybir as mybir
from concourse.bass_utils import run_bass_kernel_spmd

F32 = mybir.dt.float32
BF16 = mybir.dt.bfloat16
AF = mybir.ActivationFunctionType
ALU = mybir.AluOpType
AX = mybir.AxisListType

D = 1024
SEQ = 8192
CTX = 256
NT = 66
NOWN = 16
EPS = 1e-6
BIGD = 1.0e6
ENGS = ("sp", "act", "dve", "pool", "pe")

WA_COLS = 1536
WB_COLS = 1792
STRICT_SAME_ENGINE = os.environ.get("K_STRICT", "1") == "1"


class Op:
    __slots__ = ("eng", "fn", "deps", "pos", "need_inc", "ordinal", "is_dma",
                 "sem_key", "cum", "waits", "known", "name", "all_deps", "cost", "lat", "idx", "fin", "nsucc", "succ", "slack")


class Prog:
    XLAT = float(os.environ.get("K_XLAT", "0.1"))

    def __init__(self):
        self.ops = {e: [] for e in ENGS}
        self.all = []
        self.segments = [[]]
        self.last_writer = {}
        self.readers = {}
        self.dma_counts = {}
        self.group_keys = set()
        self.slack = {}

    def add(self, eng, fn, reads=(), writes=(), dma_key=None, name="", cost=0.1, lat=0.0):
        op = Op()
        op.eng = eng
        op.fn = fn
        op.name = name
        op.is_dma = dma_key is not None
        op.sem_key = dma_key
        op.need_inc = False
        op.ordinal = 0
        op.waits = []
        op.known = None
        op.cum = 0
        op.cost = cost
        op.lat = lat
        op.idx = len(self.all)
        op.slack = self.slack.get(eng, 0.0)
        bkr = [r for r in reads if isinstance(r, str) and r.startswith("bk")]
        if bkr:
            writes = list(writes) + [r for r in bkr if r not in writes]
        deps = {}
        for r in reads:
            w = self.last_writer.get(r)
            if w is not None:
                deps[id(w)] = (w, "raw")
        for wr in writes:
            w = self.last_writer.get(wr)
            if w is not None and id(w) not in deps:
                deps[id(w)] = (w, "waw")
            for rd in self.readers.get(wr, ()):
                if id(rd) not in deps:
                    deps[id(rd)] = (rd, "war")
        op.all_deps = [(w, kind) for (w, kind) in deps.values() if w is not op]
        for r in reads:
            self.readers.setdefault(r, []).append(op)
        for wr in writes:
            self.last_writer[wr] = op
            self.readers[wr] = []
        self.all.append(op)
        self.segments[-1].append(op)
        return op

    def barrier(self):
        self.segments.append([])

    def _schedule_segment(self, seg, t0):
        inseg = set(id(o) for o in seg)
        for o in seg:
            o.succ = []
        npred = {}
        for o in seg:
            n = 0
            for d, _k in o.all_deps:
                if id(d) in inseg:
                    d.succ.append(o)
                    n += 1
            npred[id(o)] = n
        blv = {}
        for o in reversed(seg):
            m = 0.0
            for sct in o.succ:
                v = blv[id(sct)]
                if v > m:
                    m = v
            blv[id(o)] = m + o.cost + o.lat
        use_bl = os.environ.get("K_PRIO", "bl") == "bl"
        ready = {e: [] for e in ENGS}
        rtime = {}
        for o in seg:
            if npred[id(o)] == 0:
                ready[o.eng].append(o)
                rtime[id(o)] = t0
        free = {e: t0 for e in ENGS}
        out = []
        remaining = len(seg)
        tend = t0
        while remaining:
            best = None
            for e in ENGS:
                if not ready[e]:
                    continue
                fe = free[e]
                cand = None
                for o in ready[e]:
                    st = rtime[id(o)]
                    if st < fe:
                        st = fe
                    if use_bl:
                        key = (st, -blv[id(o)], o.idx)
                    else:
                        key = (st, o.idx)
                    if cand is None or key < cand[0]:
                        cand = (key, o)
                if best is None or cand[0] < best[0]:
                    best = cand
            o = best[1]
            st = best[0][0]
            ready[o.eng].remove(o)
            free[o.eng] = st + o.cost
            o.fin = st + o.cost + o.lat
            if o.fin > tend:
                tend = o.fin
            out.append(o)
            remaining -= 1
            for sct in o.succ:
                k = id(sct)
                npred[k] -= 1
                t = o.fin + (self.XLAT if (sct.eng != o.eng or o.is_dma) else 0.0)
                t += sct.slack
                if rtime.get(k, t0) < t:
                    rtime[k] = t
                if npred[k] == 0:
                    ready[sct.eng].append(sct)
        return out, tend

    def resolve(self):
        order = []
        t = 0.0
        nseg = len(self.segments)
        for si, seg in enumerate(self.segments):
            tprev = t
            sched, t = self._schedule_segment(seg, t)
            order += sched
            if os.environ.get("K_VERBOSE"):
                busy = {e: 0.0 for e in ENGS}
                for o in sched:
                    busy[o.eng] += o.cost
                print("segment %d: %.1f us (%.1f -> %.1f) busy %s" % (si, t - tprev, tprev, t, {e: round(v, 1) for e, v in busy.items()}))
            if si < nseg - 1:
                lasts = {}
                dm = {}
                for o in sched:
                    if o.is_dma:
                        dm[o.sem_key] = o
                    else:
                        lasts[o.eng] = o
                bl = list(lasts.values()) + list(dm.values())
                for e in ENGS:
                    b = Op()
                    b.eng = e
                    b.fn = lambda eng: eng.nop()
                    b.name = "barrier"
                    b.is_dma = False
                    b.sem_key = None
                    b.need_inc = False
                    b.ordinal = 0
                    b.waits = []
                    b.known = None
                    b.cum = 0
                    b.cost = 0.05
                    b.lat = 0
                    b.idx = -1
                    b.slack = 0.0
                    b.all_deps = [(o, "raw") for o in bl]
                    order.append(b)
                t += 1.0
        self.sched_time = t
        self.dma_counts = {}
        self.ops = {e: [] for e in ENGS}
        for o in order:
            if o.is_dma:
                c = self.dma_counts.get(o.sem_key, 0) + 1
                self.dma_counts[o.sem_key] = c
                o.cum = c
            o.pos = len(self.ops[o.eng])
            self.ops[o.eng].append(o)
        self.all = order
        for o in order:
            o.deps = []
            for w, kind in o.all_deps:
                if (not w.is_dma) and (not o.is_dma) and w.eng == o.eng:
                    if o.eng == "pe" or (kind != "raw" and not STRICT_SAME_ENGINE):
                        continue
                o.deps.append(w)
        prev_known = {e: {} for e in ENGS}
        for op in order:
            known = dict(prev_known[op.eng])
            items = []
            for d in op.deps:
                if d.is_dma:
                    val = self.dma_counts[d.sem_key] if d.sem_key in self.group_keys else d.cum
                    items.append((("dma", d.sem_key), val, d))
                else:
                    items.append((d.eng, d.pos + 1, d))
            items.sort(key=lambda x: -x[1])
            for ck, val, d in items:
                if known.get(ck, 0) >= val:
                    continue
                op.waits.append((d, val))
                d.need_inc = True
                known[ck] = val
                if d.known:
                    for k, v in d.known.items():
                        if known.get(k, 0) < v:
                            known[k] = v
            op.known = known
            prev_known[op.eng] = known
        for e in ENGS:
            n = 0
            for op in self.ops[e]:
                if op.is_dma:
                    continue
                if op.need_inc:
                    n += 1
                    op.ordinal = n

    def emit(self, block, sems, dma_sems):
        prog = self

        def run(engname):
            def body(eng):
                for op in prog.ops[engname]:
                    for d, val in op.waits:
                        if d.is_dma:
                            eng.wait_ge(dma_sems[d.sem_key], 16 * val)
                        else:
                            eng.wait_ge(sems[d.eng], d.ordinal)
                    ins = op.fn(eng)
                    if op.is_dma:
                        ins.then_inc(dma_sems[op.sem_key], 16)
                    elif op.need_inc:
                        ins.then_inc(sems[engname], 1)
            return body

        block.sync(run("sp"))
        block.scalar(run("act"))
        block.vector(run("dve"))
        block.gpsimd(run("pool"))
        block.tensor(run("pe"))


def bc(ap, axis, n):
    u = ap.unsqueeze(axis)
    shp = list(u.shape)
    shp[axis] = n
    return u.to_broadcast(shp)


def build_program(debug=None):
    debug = debug or {}
    nc = bass.Bass("TRN2", target_bir_lowering=False)

    def din(name, shape):
        return nc.dram_tensor(name, list(shape), F32, kind="ExternalInput").ap()

    xo = din("xo", [NT * 128, D])
    tab = din("tab", [NT, 128, 320])
    dist = din("dist", [128, 100])
    cv = din("cv", [128, 16])
    w_mod = din("w_mod", [D, 3 * D])
    b_mod2 = din("b_mod2", [2, 3 * D])
    norm_w2 = din("norm_w2", [2, D])
    w_in = din("w_in", [D, 3328])
    qkn = din("qkn", [128, 128])
    dec = din("dec", [128, 8])
    gnw = din("gnw", [128, 512])
    fnw = din("fnw", [128, D])
    w_out = din("w_out", [D, D])
    ident = din("ident", [128, 128])
    cst = din("cst", [128, 898])
    sel = din("sel", [2, 256])
    y = nc.dram_tensor("y", [NOWN * 128, D], F32, kind="ExternalOutput").ap()
    dbg_out = {}
    for name, shape in debug.items():
        dbg_out[name] = nc.dram_tensor("dbg_" + name, list(shape), F32, kind="ExternalOutput").ap()

    P = Prog()
    P.group_keys.add("const")

    with ExitStack() as es:
        def sb(name, shape, dt=F32):
            return es.enter_context(nc.sbuf_tensor(name, list(shape), dt))

        identf = sb("identf", [128, 128])
        identb = sb("identb", [128, 128], BF16)
        kT = sb("kT", [128, NT * 128], BF16)
        Vaug = sb("Vaug", [128, NT, 2, 65], BF16)
        gate_bc = sb("gate_bc", [128, D])
        g_bc = sb("g_bc", [128, D])
        shift_bc = sb("shift_bc", [128, D])
        arenaA = sb("arenaA", [128, 8 * WA_COLS], BF16)
        arenaB = sb("arenaB", [128, 16 * 1024], BF16)
        arenaR = sb("arenaR", [128, 24576], BF16)
        stage = sb("stage", [128, 4, D])
        hb = sb("hb", [128, 2, D], BF16)
        xT = sb("xT", [128, 2, 8, 128], BF16)
        tabs = sb("tabs", [128, 2, 320])
        junk = sb("junk", [128, D], BF16)
        sm = sb("sm", [128, 128])
        qkn_s = sb("qkn_s", [128, 128])
        dec_s = sb("dec_s", [128, 8])
        lg = sb("lg", [128, 8])
        cvs = sb("cvs", [128, 16])
        scv = sb("scv", [128, 16])
        cvt = sb("cvt", [128, 16])
        sel_s = sb("sel_s", [2, 256])
        dist_s = sb("dist_s", [128, 100])
        wts = sb("wts", [128, 50, 2, 4])
        DT = sb("DT", [128, 4, 128])
        QDF = sb("QDF", [128, 4, 128], BF16)
        QDB = sb("QDB", [128, 4, 128], BF16)
        kdec = sb("kdec", [128, 8])
        gnw_s = sb("gnw_s", [128, 512])
        Sf0 = sb("Sf0", [128, 512])
        Sb0 = sb("Sb0", [128, 512])
        sqt = sb("sqt", [128, 512])
        kn = sb("kn", [128, 128])
        m1a = sb("m1a", [128, 128])
        m2a = sb("m2a", [128, 128])
        kro = sb("kro", [128, 128], BF16)
        m1r = sb("m1r", [128, 512])
        m2r = sb("m2r", [128, 512])
        krf = sb("krf", [128, 512])
        kwf = sb("kwf", [128, 512], BF16)
        kwb = sb("kwb", [128, 512], BF16)
        vrb = sb("vrb", [128, 512], BF16)
        b512 = sb("b512", [128, 512], BF16)
        q512 = sb("q512", [128, 512], BF16)
        dummy = sb("mk_dummy", [128, 8])
        shiftT = sb("shiftT", [128, 16])

        psum = es.enter_context(nc.psum_tensor("psum", [128, 8 * 512], F32))

        def bank(i, n=1):
            return psum[:, i * 512:(i + n) * 512]

        def bankbf(i):
            return psum[:, i * 512:(i + 1) * 512].bitcast(BF16)

        W_A = arenaA[:, :].rearrange("p (k c) -> p k c", k=8)
        W_B = arenaB[:, 0:8 * WB_COLS].rearrange("p (k c) -> p k c", k=8)
        mixed = arenaB[:, :].rearrange("p (t c) -> p t c", t=16)
        wo = arenaA[:, 0:8192].rearrange("p (k c) -> p k c", k=8)
        mT = arenaA[:, 8192:8192 + 4096].rearrange("p (b k c) -> p b k c", b=4, k=8)
        qrT = arenaR[:, 0:8192].rearrange("p (h t) -> p h t", h=4)
        kr = arenaR[:, 8192:16384].rearrange("p (t c) -> p t c", t=16)
        vr = arenaR[:, 16384:24576].rearrange("p (t c) -> p t c", t=16)
        qT = arenaR[:, 0:8192].rearrange("p (h t) -> p h t", h=4)
        ga = arenaR[:, 8192:16384].rearrange("p (t c) -> p t c", t=16)
        NPT = int(os.environ.get("K_NPT", "4"))
        PT = arenaR[:, 16384:16384 + 1024 * NPT].rearrange("p (b c) -> p b c", b=NPT)
        R32 = arenaR[:, :].bitcast(F32)
        modrows = R32[0:2, 0:3072]
        bm2 = R32[0:2, 3072:6144]
        grow = R32[0:2, 6144:7168]
        nw2 = R32[0:2, 7168:8192]
        g_bc_c = R32[:, 8192:9216]
        shift_bc_c = R32[:, 9216:10240]
        cst_s = R32[:, 10240:10240 + 898]
        tmp1 = R32[0:2, 11264:12288]
        SbN = stage[:, :, :].bitcast(BF16).rearrange("p s (a c) -> p (s a) c", a=4)

        sems = {e: es.enter_context(nc.semaphore("s_" + e)) for e in ENGS}
        dma_keys = ["const", "st0", "st1", "st2", "st3", "tab0", "tab1", "out0", "out1", "out2", "out3", "dbg", "fnw"]
        dsem = {k: es.enter_context(nc.semaphore("d_" + k)) for k in dma_keys}

        def nfree(ap):
            n = 1
            for d in ap.shape[1:]:
                n *= d
            return n

        def ecost(eng, ap, f32=True):
            n = nfree(ap)
            if eng == "act":
                return (n + 150) / 1200.0
            if eng == "dve":
                return n / 900.0 + 0.1
            if eng == "pool":
                return n * 1.8 / 1000.0 + 0.3
            return 0.1

        def dma(eng, out, in_, key, reads=(), writes=(), name=""):
            nbytes = nfree(out) * out.shape[0] * (4 if out.dtype == F32 else 2)
            return P.add(eng, lambda e, o=out, i=in_: e.dma_start(out=o, in_=i), reads=reads, writes=writes,
                         dma_key=key, name=name, cost=0.08, lat=2.0 + nbytes / 1.5e5)

        def act(out, in_, func, reads, writes, scale=1.0, bias=0.0, accum=None, name=""):
            def fn(e, out=out, in_=in_, func=func, scale=scale, bias=bias, accum=accum):
                kw = {}
                if accum is not None:
                    kw["accum_out"] = accum
                return e.activation(out=out, in_=in_, func=func, bias=bias, scale=scale, **kw)
            return P.add("act", fn, reads=reads, writes=writes, name=name, cost=ecost("act", in_) + (0.1 if accum is not None else 0))

        def tt(eng, out, in0, in1, op, reads, writes, name=""):
            return P.add(eng, lambda e, o=out, a=in0, b=in1, op=op: e.tensor_tensor(out=o, in0=a, in1=b, op=op),
                         reads=reads, writes=writes, name=name, cost=ecost(eng, out))

        def ts(eng, out, in0, s1, s2, op0, op1, reads, writes, name=""):
            def fn(e, out=out, in0=in0, s1=s1, s2=s2, op0=op0, op1=op1):
                if op1 is None:
                    return e.tensor_scalar(out=out, in0=in0, scalar1=s1, scalar2=None, op0=op0)
                return e.tensor_scalar(out=out, in0=in0, scalar1=s1, scalar2=s2, op0=op0, op1=op1)
            return P.add(eng, fn, reads=reads, writes=writes, name=name, cost=ecost(eng, out))

        def stt(out, in0, scalar, in1, op0, op1, reads, writes, name=""):
            return P.add("dve", lambda e, o=out, a=in0, s=scalar, b=in1, op0=op0, op1=op1:
                         e.scalar_tensor_tensor(out=o, in0=a, scalar=s, in1=b, op0=op0, op1=op1),
                         reads=reads, writes=writes, name=name, cost=ecost("dve", out) * 1.2)

        def cp(eng, out, in_, reads, writes, name=""):
            if eng == "act":
                return P.add("act", lambda e, o=out, i=in_: e.copy(out=o, in_=i), reads=reads, writes=writes, name=name,
                             cost=ecost("act", out))
            return P.add(eng, lambda e, o=out, i=in_: e.tensor_copy(out=o, in_=i), reads=reads, writes=writes, name=name,
                         cost=ecost(eng, out))

        def mm(out, lhsT, rhs, start, stop, reads, writes, name="", skip=False):
            def fn(e, out=out, lhsT=lhsT, rhs=rhs, start=start, stop=stop, skip=skip):
                if skip:
                    return e.matmul(out, lhsT=lhsT, rhs=rhs, start=start, stop=stop, skip_group_check=True)
                return e.matmul(out, lhsT=lhsT, rhs=rhs, start=start, stop=stop)
            n = max(nfree(rhs), 64)
            c = n / 2400.0 + 0.025
            if lhsT.shape[0] <= 64:
                c *= 0.65
            if rhs.dtype == F32:
                c *= 4
            return P.add("pe", fn, reads=reads, writes=writes, name=name, cost=c)

        def tr(out, in_, reads, writes, name=""):
            return P.add("pe", lambda e, o=out, i=in_: e.transpose(out=o, in_=i, identity=identb[:]),
                         reads=list(reads) + ["identb"], writes=writes, name=name, cost=0.09)

        def recip(out, in_, reads, writes):
            return P.add("dve", lambda e, o=out, i=in_: e.reciprocal(out=o, in_=i), reads=reads, writes=writes,
                         cost=ecost("dve", out))

        def rsqrt_chain(out, in_, scale, reads_in, key):
            act(out, in_, AF.Ln, reads=reads_in, writes=[key], scale=scale, bias=EPS)
            act(out, out, AF.Exp, reads=[key], writes=[key], scale=-0.5)

        def dump(name, ap_src, reads):
            if name not in dbg_out:
                return
            dma("pool" if ap_src.dtype != F32 else "sp", dbg_out[name], ap_src, "dbg", reads=reads, writes=["dbg_" + name])

        for (dst, src, res) in [
            (cvs[:], cv, "cvs"), (bm2, b_mod2, "bm2"), (nw2, norm_w2, "nw2"), (qkn_s[:], qkn, "qkn"),
            (dec_s[:], dec, "dec"), (cst_s, cst, "cst"), (sel_s[:], sel, "sel"), (identf[:], ident, "identf"),
            (dist_s[:], dist, "dist"), (gnw_s[:], gnw, "gnw"),
        ]:
            dma("sp", dst, src, "const", writes=[res])

        cp("dve", identb[:], identf[:], ["identf"], ["identb"])
        P.add("pool", lambda e: e.memset(Vaug[:, :, :, 64:65], 1.0), writes=["Vones"])

        act(lg[:], dec_s[:], AF.Exp, ["dec"], ["lg"], scale=-1.0)
        act(lg[:], lg[:], AF.Ln, ["lg"], ["lg"], scale=1.0, bias=1.0)
        ts("dve", lg[:], lg[:], -1.0, None, ALU.mult, None, ["lg"], ["lg"])
        c127 = cst_s[:, 0:1]
        cpp = cst_s[:, 1:2]
        act(kdec[:, 0:4], lg[:, 0:4], AF.Exp, ["lg", "cst"], ["kdecf"], scale=c127)
        act(kdec[:, 4:8], lg[:, 4:8], AF.Exp, ["lg", "cst"], ["kdecb"], scale=cpp)
        act(sm[:, 0:8], lg[:, 0:8], AF.Exp, ["lg"], ["cd8"], scale=128.0)
        for h in range(4):
            act(QDF[:, h, :], cst_s[:, 2:130], AF.Exp, ["lg", "cst"], ["QDF%d" % h], scale=lg[:, h:h + 1])
            act(QDB[:, h, :], cst_s[:, 130:258], AF.Exp, ["lg", "cst"], ["QDB%d" % h], scale=lg[:, 4 + h:5 + h])
            act(sqt[:, 0:128], cst_s[:, 258:386], AF.Exp, ["lg", "cst"], ["sqt"], scale=lg[:, h:h + 1])
            act(sqt[:, 128:256], cst_s[:, 386:514], AF.Exp, ["lg", "cst"], ["sqt"], scale=lg[:, 4 + h:5 + h])
            tt("dve", sqt[:, 0:256], sqt[:, 0:256], cst_s[:, 514:770], ALU.mult, ["sqt", "cst"], ["sqt"])
            tt("dve", sqt[:, 0:128], sqt[:, 0:128], sqt[:, 128:256], ALU.add, ["sqt"], ["sqt"])
            tt("dve", DT[:, h, :], sqt[:, 0:128], cst_s[:, 770:898], ALU.add, ["sqt", "cst"], ["DT%d" % h])
        dist3 = dist_s[:].rearrange("p (j d) -> p j d", d=2)
        for d_ in range(2):
            for h in range(4):
                ts("dve", wts[:, :, d_, h], dist3[:, :, d_], lg[:, d_ * 4 + h:d_ * 4 + h + 1], None, ALU.mult, None,
                   ["dist", "lg"], ["wts_%d%d" % (d_, h)])
        wts_keys = ["wts_%d%d" % (d_, h) for d_ in range(2) for h in range(4)]
        wflat = wts[:].rearrange("p j d h -> p (j d h)")
        act(wflat, wflat, AF.Exp, wts_keys, ["wts"])

        act(cvt[:], cvs[:], AF.Exp, ["cvs"], ["cvt"], scale=-1.0)
        ts("dve", cvt[:], cvt[:], 1.0, None, ALU.add, None, ["cvt"], ["cvt"])
        recip(cvt[:], cvt[:], ["cvt"], ["cvt"])
        tt("dve", scv[:], cvs[:], cvt[:], ALU.mult, ["cvs", "cvt"], ["scv"])
        scv3 = scv[:].rearrange("p (k w) -> p k w", w=2)

        slot_ctr = [0]

        def next_slot():
            s = slot_ctr[0] % 4
            slot_ctr[0] += 1
            return s

        for k in range(8):
            for j in range(3):
                s = next_slot()
                dma("sp", stage[:, s, :], w_mod[k * 128:(k + 1) * 128, j * 1024:(j + 1) * 1024], "st%d" % s,
                    writes=["st%d" % s])
                for i in range(2):
                    bnk = j * 2 + i
                    mm(psum[0:2, bnk * 512:(bnk + 1) * 512], scv3[:, k, :], stage[:, s, i * 512:(i + 1) * 512],
                       k == 0, k == 7, ["scv", "st%d" % s], ["bk%d" % bnk])
        for bnk in range(6):
            tt("dve", modrows[:, bnk * 512:(bnk + 1) * 512], psum[0:2, bnk * 512:(bnk + 1) * 512],
               bm2[:, bnk * 512:(bnk + 1) * 512], ALU.add, ["bk%d" % bnk, "bm2"], ["modrows%d" % bnk])
        ts("dve", tmp1, modrows[:, 1024:2048], 1.0, None, ALU.add, None, ["modrows2", "modrows3"], ["tmp1"])
        tt("dve", grow, tmp1, nw2, ALU.mult, ["tmp1", "nw2"], ["grow"])
        bjobs = [
            (grow, 0, g_bc[:], ["grow"], "g_bc"),
            (grow, 1, g_bc_c, ["grow"], "g_bc_c"),
            (modrows[:, 0:1024], 0, shift_bc[:], ["modrows0", "modrows1"], "shift_bc"),
            (modrows[:, 0:1024], 1, shift_bc_c, ["modrows0", "modrows1"], "shift_bc_c"),
            (modrows[:, 2048:3072], 0, gate_bc[:], ["modrows4", "modrows5"], "gate_bc"),
        ]
        bi = 0
        for (row, which, dst, rds, res) in bjobs:
            for c in range(2):
                bnk = 6 + (bi % 2)
                bi += 1
                mm(bank(bnk), sel_s[0:2, which * 128:(which + 1) * 128], row[:, c * 512:(c + 1) * 512], True, True,
                   rds + ["sel"], ["bk%d" % bnk])
                cp("act" if bi % 2 else "dve", dst[:, c * 512:(c + 1) * 512], bank(bnk), ["bk%d" % bnk],
                   [res + str(c)])
        GB = ["g_bc0", "g_bc1"]
        GBC = ["g_bc_c0", "g_bc_c1"]
        SBK = ["shift_bc0", "shift_bc1"]
        SBC = ["shift_bc_c0", "shift_bc_c1"]
        GATE = ["gate_bc0", "gate_bc1"]
        for k in range(8):
            P.add("pe", lambda e, k=k: e.transpose(out=psum[:, 7 * 512 + 2 * k:7 * 512 + 2 * k + 2],
                                                   in_=modrows[:, k * 128:(k + 1) * 128], identity=identf[0:2, 0:2]),
                  reads=["modrows0", "modrows1", "identf"], writes=["bk7"], cost=0.3)
        cp("dve", shiftT[:, :], psum[:, 7 * 512:7 * 512 + 16], ["bk7"], ["shiftT"])
        shiftT3 = shiftT[:, :].rearrange("p (k w) -> p k w", w=2)

        pieces = [(0, 1024, "A", 0), (1024, 1536, "A", 1024), (1536, 2560, "B", 0), (2560, 3328, "B", 1024)]
        ci = 0
        for k in range(8):
            for (c0, c1, which, off) in pieces:
                s = next_slot()
                n = c1 - c0
                dma("sp", stage[:, s, 0:n], w_in[k * 128:(k + 1) * 128, c0:c1], "st%d" % s, writes=["st%d" % s])
                dstW = W_A if which == "A" else W_B
                eng = ("pool", "dve", "act")[ci % 3]
                ci += 1
                cp(eng, dstW[:, k, off:off + n], stage[:, s, 0:n], ["st%d" % s], ["W%s" % which])

        xctr = [0]

        def xproc(t, ctx_tile):
            i = xctr[0] % 2
            xctr[0] += 1
            s = next_slot()
            st = "st%d" % s
            dma("sp", stage[:, s, :], xo[t * 128:(t + 1) * 128, :], st, writes=[st])
            dma("sp", tabs[:, i, :], tab[t], "tab%d" % i, writes=["tabs%d" % i])
            ssq = sm[:, 8 + i:9 + i]
            act(hb[:, i, :], stage[:, s, :], AF.Square, [st], ["ssq%d" % i, "hb%d" % i], accum=ssq)
            rsqrt_chain(ssq, ssq, 1.0 / D, ["ssq%d" % i], "ssq%d" % i)
            gsrc, grd = (g_bc_c, GBC) if ctx_tile else (g_bc[:], GB)
            stt(hb[:, i, :], stage[:, s, :], ssq, gsrc, ALU.mult, ALU.mult, [st, "ssq%d" % i] + grd, ["hb%d" % i])
            pb = bankbf(i)
            for k in range(8):
                tr(pb[:, k * 128:(k + 1) * 128], hb[:, i, k * 128:(k + 1) * 128], ["hb%d" % i], ["bk%d" % i])
            tt("dve", xT[:, i, :, :], pb.rearrange("p (k c) -> p k c", k=8), bc(shiftT3[:, :, 1 if ctx_tile else 0], 2, 128),
               ALU.add, ["bk%d" % i, "shiftT"], ["xT%d" % i])
            return i

        def inproj(i, Wv, c0, n, bnk, wres):
            for k in range(8):
                mm(psum[:, bnk * 512:bnk * 512 + n], xT[:, i, k, :], Wv[:, k, c0:c0 + n], k == 0, k == 7,
                   ["xT%d" % i, wres], ["bk%d" % bnk])

        def rope(src, nh, hd, cosb, sinb, dst, tmpa, tmpb, rd_src, rd_tab, wr, ra, rb, eng_mul=("dve", "dve"),
                 eng_add=tuple(os.environ.get("K_ROPEADD", "dve,pool").split(","))):
            n = nh * hd
            s4 = src.rearrange("p (h two d) -> p h two d", h=nh, two=2)
            cb = bc(bc(cosb, 1, 2), 1, nh)
            sbb = bc(bc(sinb, 1, 2), 1, nh)
            a4 = tmpa[:, 0:n].rearrange("p (h two d) -> p h two d", h=nh, two=2)
            b4 = tmpb[:, 0:n].rearrange("p (h two d) -> p h two d", h=nh, two=2)
            d4 = dst.rearrange("p (h two d) -> p h two d", h=nh, two=2)
            tt(eng_mul[0], a4, s4, cb, ALU.mult, rd_src + rd_tab, ra)
            tt(eng_mul[1], b4, s4, sbb, ALU.mult, rd_src + rd_tab, rb)
            tt(eng_add[0], d4[:, :, 0, :], a4[:, :, 0, :], b4[:, :, 1, :], ALU.subtract, ra + rb, [wr + "_lo"])
            tt(eng_add[1], d4[:, :, 1, :], a4[:, :, 1, :], b4[:, :, 0, :], ALU.add, ra + rb, [wr + "_hi"])

        first_state = [True]
        ON4 = ["on0", "on1", "on2", "on3"]

        def kv_epilogue(t, i, bnk):
            tb = "tabs%d" % i
            bk = "bk%d" % bnk
            ka = psum[:, bnk * 512:bnk * 512 + 128]
            va = psum[:, bnk * 512 + 128:bnk * 512 + 256]
            act(sqt[:, 0:128], ka, AF.Square, [bk], ["sqt"])
            P.add("dve", lambda e: e.tensor_reduce(out=sm[:, 16:18], in_=sqt[:, 0:128].rearrange("p (h d) -> p h d", h=2),
                                                   axis=AX.X, op=ALU.add), reads=["sqt"], writes=["rk"])
            rsqrt_chain(sm[:, 16:18], sm[:, 16:18], 1.0 / 64, ["rk"], "rk")
            for h in range(2):
                stt(kn[:, h * 64:(h + 1) * 64], ka[:, h * 64:(h + 1) * 64], sm[:, 16 + h:17 + h], qkn_s[:, 64:128],
                    ALU.mult, ALU.mult, [bk, "rk", "qkn"], ["kn%d" % h])
            rope(kn[:, :], 2, 64, tabs[:, i, 0:32], tabs[:, i, 32:64], kro[:, :], m1a, m2a, ["kn0", "kn1"], [tb], "kro", ["m1a"], ["m2a"])
            pb = bankbf(5)
            tr(pb[:, 0:128], kro[:, :], ["kro_lo", "kro_hi"], ["bk5"])
            cp("act", kT[:, t * 128:(t + 1) * 128], pb[:, 0:128], ["bk5"], ["kT"])
            cp("act", Vaug[:, t, :, 0:64], va.rearrange("p (g d) -> p g d", g=2), [bk], ["Vaug"])

        def kr_rope(i, bnk, dst, wr):
            rope(psum[:, bnk * 512:(bnk + 1) * 512], 4, 128, tabs[:, i, 192:256], tabs[:, i, 256:320], dst, m1r, m2r,
                 ["bk%d" % bnk], ["tabs%d" % i], wr, ["m1r"], ON4)

        def other_tile(t, ctx_tile):
            j = t - NOWN
            i = xproc(t, ctx_tile)
            inproj(i, W_B, 0, 256, 2, "WB")
            kv_epilogue(t, i, 2)
            inproj(i, W_B, 768, 512, 3, "WB")
            kr_rope(i, 3, krf[:, :], "krf")
            inproj(i, W_B, 1280, 512, 4, "WB")
            cp("act", vrb[:, :], bank(4), ["bk4"], ["vrb"])
            k3 = krf[:, :].rearrange("p (h d) -> p h d", h=4)
            for h in range(4):
                hs = slice(h * 128, (h + 1) * 128)
                act(kwf[:, hs], krf[:, hs], AF.Copy, ["krf_lo", "krf_hi", "wts"], ["kwf"], scale=wts[:, j, 0, h:h + 1])
                act(kwb[:, hs], krf[:, hs], AF.Copy, ["krf_lo", "krf_hi", "wts"], ["kwb"], scale=wts[:, j, 1, h:h + 1])
            for (kw, res, bnk) in ((kwf, "kwf", 6), (kwb, "kwb", 7)):
                for h in range(4):
                    st_ = first_state[0] and h == 0
                    mm(psum[:, bnk * 512 + h * 128:bnk * 512 + (h + 1) * 128], kw[:, h * 128:(h + 1) * 128],
                       vrb[:, h * 128:(h + 1) * 128], st_, False, [res, "vrb"], ["bk%d" % bnk], skip=True)
            first_state[0] = False

        def own_tile_p1(t):
            i = xproc(t, False)
            inproj(i, W_B, 0, 256, 2, "WB")
            kv_epilogue(t, i, 2)
            inproj(i, W_B, 768, 512, 3, "WB")
            kr_rope(i, 3, kr[:, t, :], "kr%d" % t)
            inproj(i, W_B, 1280, 512, 4, "WB")
            cp("act", vr[:, t, :], bank(4), ["bk4"], ["vr%d" % t])
            inproj(i, W_B, 256, 512, 3, "WB")
            rope(bank(3), 4, 128, tabs[:, i, 64:128], tabs[:, i, 128:192], b512[:, :], m1r, m2r, ["bk3"],
                 ["tabs%d" % i], "b512", ["m1r"], ON4)
            pb = bankbf(5)
            for h in range(4):
                tr(pb[:, 512 + h * 128:512 + (h + 1) * 128], b512[:, h * 128:(h + 1) * 128], ["b512_lo", "b512_hi"],
                   ["bk5"])
            cp("act", qrT[:, :, t * 128:(t + 1) * 128], pb[:, 512:1024].rearrange("p (h c) -> p h c", h=4), ["bk5"],
               ["qrT%d" % t])

        order = [64, 65] + list(range(16, 64))
        lim = int(os.environ.get("K_NOTH", "50"))
        order = order[:lim]
        for t in order:
            other_tile(t, t >= 64)
        if order:
            cp("act", Sf0[:, :], bank(6), ["bk6"], ["Sf0"])
            cp("dve", Sb0[:, :], bank(7), ["bk7"], ["Sb0"])
        dump("kT", kT[:, :], ["kT"])
        dump("Vaug", Vaug[:].rearrange("p t g d -> p (t g d)"), ["Vaug"])
        dump("Sf0", Sf0[:, :], ["Sf0"])
        dump("Sb0", Sb0[:, :], ["Sb0"])
        dump("wts", wts[:].rearrange("p j d h -> p (j d h)"), ["wts"])
        dump("gbc", g_bc[:], GB)
        dump("sbc", shift_bc[:], SBK)
        dump("gbcc", g_bc_c, GBC)
        dump("sbcc", shift_bc_c, SBC)
        dump("gate", gate_bc[:], GATE)
        dump("DT", DT[:].rearrange("p h c -> p (h c)"), ["DT0", "DT1", "DT2", "DT3"])
        P.barrier()


        def bkres(n):
            return ["bk5", "bk5"] if n == 5 else ["bk%d" % n]

        nown = int(os.environ.get("K_NOWN", "16"))
        stop_after = os.environ.get("K_STOP", "")

        own_order = list(range(nown))
        if os.environ.get("K_OWNREV", "1") == "1":
            own_order.reverse()
        for t in own_order:
            own_tile_p1(t)
        KR = lambda t: ["kr%d_lo" % t, "kr%d_hi" % t]
        dump("qrT", qrT.rearrange("p h t -> p (h t)"), ["qrT%d" % t for t in range(nown)])
        dump("kr", kr.rearrange("p t c -> p (t c)"), [x for t in range(nown) for x in KR(t)])
        dump("vr", vr.rearrange("p t c -> p (t c)"), ["vr%d" % t for t in range(nown)])
        P.add("pool", lambda e: e.memset(dummy[:, 0:2], 0.0), reads=[], writes=["WB", "WB_done", "dummy0"], cost=0.1)

        krf_bf = krf[:, :].bitcast(BF16)
        qf_s = krf_bf[:, 0:512]
        qb_s = krf_bf[:, 512:1024]
        DTf = DT[:].rearrange("p h c -> p (h c)")
        if stop_after != "p1":
            for n in range(nown - 1, -1, -1):
                cp("act", SbN[:, n, :], Sb0[:, :], ["Sb0"], ["st%d" % (n // 4)])
                tt("pool", kwb[:, :].rearrange("p (h d) -> p h d", h=4), kr[:, n, :].rearrange("p (h d) -> p h d", h=4),
                   bc(kdec[:, 4:8], 2, 128), ALU.mult, KR(n) + ["kdecb"], ["kwb"])
                for h in range(4):
                    mm(psum[:, 6 * 512 + h * 128:6 * 512 + (h + 1) * 128], kwb[:, h * 128:(h + 1) * 128],
                       vr[:, n, h * 128:(h + 1) * 128], True, True, ["kwb", "vr%d" % n], ["bk6"])
                for h in range(4):
                    stt(Sb0[:, h * 128:(h + 1) * 128], Sb0[:, h * 128:(h + 1) * 128], sm[:, 4 + h:5 + h],
                        psum[:, 6 * 512 + h * 128:6 * 512 + (h + 1) * 128], ALU.mult, ALU.add, ["Sb0", "cd8", "bk6"], ["Sb0"])
            for n in range(nown):
                tsl = slice(n * 128, (n + 1) * 128)
                cp("act", b512[:, :], Sf0[:, :], ["Sf0"], ["b512_lo", "b512_hi"])
                pb = bankbf(5)
                for h in range(4):
                    tr(pb[:, h * 128:(h + 1) * 128], kr[:, n, h * 128:(h + 1) * 128], KR(n), ["bk5"])
                cp("act", vrb[:, :], pb[:, 0:512], ["bk5"], ["krT_s"])
                for h in range(4):
                    mm(psum[:, 2 * 512 + h * 128:2 * 512 + (h + 1) * 128], vrb[:, h * 128:(h + 1) * 128], qrT[:, h, tsl],
                       True, True, ["krT_s", "qrT%d" % n], ["bk2"])
                tt("dve", kwf[:, :], bank(2), DTf, ALU.mult, ["bk2", "DT0", "DT1", "DT2", "DT3"], ["msk"])
                tt(os.environ.get("K_PA", "pool"), qf_s.rearrange("p (h c) -> p h c", h=4), qrT[:, :, tsl], QDF[:], ALU.mult,
                   ["qrT%d" % n] + ["QDF%d" % h for h in range(4)], ["qf_s"])
                tt(os.environ.get("K_PA", "pool"), qb_s.rearrange("p (h c) -> p h c", h=4), qrT[:, :, tsl], QDB[:], ALU.mult,
                   ["qrT%d" % n] + ["QDB%d" % h for h in range(4)], ["qb_s"])
                for h in range(4):
                    hs = slice(h * 128, (h + 1) * 128)
                    o_ = psum[:, 3 * 512 + h * 128:3 * 512 + (h + 1) * 128]
                    mm(o_, kwf[:, hs], vr[:, n, hs], True, False, ["msk", "vr%d" % n], ["bk3"])
                    mm(o_, qf_s[:, hs], b512[:, hs], False, False, ["qf_s", "b512_lo", "b512_hi"], ["bk3"])
                    mm(o_, qb_s[:, hs], SbN[:, n, hs], False, True, ["qb_s", "st%d" % (n // 4)], ["bk3"])
                cp("act", m1r[:, :], bank(3), ["bk3"], ["m1r"])
                for h in range(4):
                    hs = slice(h * 128, (h + 1) * 128)
                    P.add("dve", lambda e, h=h, hs=hs: e.bn_stats(out=sm[:, 32 + 6 * h:38 + 6 * h], in_=m1r[:, hs]),
                          reads=["m1r"], writes=["bnst%d" % h])
                    P.add("dve", lambda e, h=h: e.bn_aggr(out=sm[:, 56 + 2 * h:58 + 2 * h], in_=sm[:, 32 + 6 * h:38 + 6 * h]),
                          reads=["bnst%d" % h], writes=["mv%d" % h])
                mv3 = sm[:, 56:64].rearrange("p (h c) -> p h c", c=2)
                act(sm[:, 64:68], mv3[:, :, 1], AF.Ln, ["mv%d" % h for h in range(4)], ["rs4"], scale=1.0, bias=EPS)
                act(sm[:, 64:68], sm[:, 64:68], AF.Exp, ["rs4"], ["rs4"], scale=-0.5)
                for h in range(4):
                    hs = slice(h * 128, (h + 1) * 128)
                    ts("dve", m2r[:, hs], m1r[:, hs], sm[:, 56 + 2 * h:57 + 2 * h], sm[:, 64 + h:65 + h], ALU.subtract,
                       ALU.mult, ["m1r", "mv%d" % h, "rs4"], ["on%d" % h])
                tt("pool", mixed[:, n, 512:1024], m2r[:, :], gnw_s[:, :], ALU.mult, ["on%d" % h for h in range(4)] + ["gnw", "WB_done"],
                   ["mixR%d" % n])
                tt("pool", kwb[:, :].rearrange("p (h d) -> p h d", h=4), kr[:, n, :].rearrange("p (h d) -> p h d", h=4),
                   bc(kdec[:, 0:4], 2, 128), ALU.mult, KR(n) + ["kdecf"], ["kwb"])
                for h in range(4):
                    mm(psum[:, 4 * 512 + h * 128:4 * 512 + (h + 1) * 128], kwb[:, h * 128:(h + 1) * 128],
                       vr[:, n, h * 128:(h + 1) * 128], True, True, ["kwb", "vr%d" % n], ["bk4"])
                for h in range(4):
                    stt(Sf0[:, h * 128:(h + 1) * 128], Sf0[:, h * 128:(h + 1) * 128], sm[:, h:h + 1],
                        psum[:, 4 * 512 + h * 128:4 * 512 + (h + 1) * 128], ALU.mult, ALU.add, ["Sf0", "cd8", "bk4"], ["Sf0"])
            dump("mixed", mixed.rearrange("p t c -> p (t c)"), ["mixR%d" % n for n in range(nown)])

        if stop_after not in ("p1", "ret"):
            for t in range(nown):
                i = xproc(t, False)
                tb = "tabs%d" % i
                inproj(i, W_A, 0, 512, 2, "WA")
                act(sqt[:, :], bank(2), AF.Square, ["bk2"], ["sqt"])
                P.add("dve", lambda e: e.tensor_reduce(out=sm[:, 72:80], in_=sqt[:, :].rearrange("p (h d) -> p h d", h=8),
                                                       axis=AX.X, op=ALU.add), reads=["sqt"], writes=["rq"])
                rsqrt_chain(sm[:, 72:80], sm[:, 72:80], 1.0 / 64, ["rq"], "rq")
                tt("dve", m1r[:, :].rearrange("p (h d) -> p h d", h=8), bank(2).rearrange("p (h d) -> p h d", h=8),
                   bc(sm[:, 72:80], 2, 64), ALU.mult, ["bk2", "rq"], ["m1r"])
                tt(os.environ.get("K_PB", "pool"), m1r[:, :].rearrange("p (h d) -> p h d", h=8), m1r[:, :].rearrange("p (h d) -> p h d", h=8),
                   bc(qkn_s[:, 0:64], 1, 8), ALU.mult, ["m1r", "qkn"], ["m1r"])
                rope(m1r[:, :], 8, 64, tabs[:, i, 0:32], tabs[:, i, 32:64], q512[:, :], m2r, krf, ["m1r"], [tb], "qro2", ["on0", "on1", "on2", "on3"], ["qf_s", "qb_s"])
                pb = bankbf(5)
                for pr in range(4):
                    tr(pb[:, 512 + pr * 128:512 + (pr + 1) * 128], q512[:, pr * 128:(pr + 1) * 128],
                       ["qro2_lo", "qro2_hi"], ["bk5"])
                cp("act", qT[:, :, t * 128:(t + 1) * 128], pb[:, 512:1024].rearrange("p (h c) -> p h c", h=4), ["bk5"],
                   ["qrT%d" % t])
                inproj(i, W_A, 512, 512, 3, "WA")
                act(sqt[:, :], bank(3), AF.Exp, ["bk3"], ["sqt"], scale=-1.0)
                act(sqt[:, :], sqt[:, :], AF.Ln, ["sqt"], ["sqt"], scale=1.0, bias=1.0)
                act(sqt[:, :], sqt[:, :], AF.Exp, ["sqt"], ["sqt"], scale=-1.0)
                tt("dve", ga[:, t, :], bank(3), sqt[:, :], ALU.mult, ["bk3", "sqt"], KR(t))
                inproj(i, W_A, 1024, 512, 4, "WA")
                act(Sb0[:, :], bank(4), AF.Exp, ["bk4"], ["Sb0"], scale=-1.0)
                act(Sb0[:, :], Sb0[:, :], AF.Ln, ["Sb0"], ["Sb0"], scale=1.0, bias=1.0)
                act(Sb0[:, :], Sb0[:, :], AF.Exp, ["Sb0"], ["Sb0"], scale=-1.0)
                tt("dve", Sb0[:, :], bank(4), Sb0[:, :], ALU.mult, ["bk4", "Sb0"], ["Sb0"])
                tt("pool", mixed[:, t, 512:1024], mixed[:, t, 512:1024], Sb0[:, :], ALU.mult, ["mixR%d" % t, "Sb0"],
                   ["mixR%d" % t])

        if stop_after not in ("p1", "ret", "p2"):
            P.add("pool", lambda e: e.memset(dummy[:, 2:4], 0.0), reads=[], writes=["WA", "WA_done", "dummy1"], cost=0.1)
            it = 0
            nkt = int(os.environ.get("K_NKT", str(NT)))
            for qb in range(nown // 4):
                for pair in range(4):
                    accb = 4
                    it += 1
                    for kt in range(nkt):
                        si = kt % 2
                        pbuf = kt % NPT
                        for g in range(2):
                            mm(bank(2 * si + g), kT[g * 64:(g + 1) * 64, kt * 128:(kt + 1) * 128],
                               qT[g * 64:(g + 1) * 64, pair, qb * 512:(qb + 1) * 512], True, True,
                               ["kT"] + ["qrT%d" % (qb * 4 + u) for u in range(4)], ["bk%d" % (2 * si + g)])
                        act(PT[:, pbuf, :], psum[:, 2 * si * 512:(2 * si + 2) * 512], AF.Exp,
                            ["bk%d" % (2 * si), "bk%d" % (2 * si + 1)], ["PT%d" % pbuf, "vr%d" % (2 * pbuf), "vr%d" % (2 * pbuf + 1)], scale=0.125)
                        for g in range(2):
                            for sub in range(4):
                                c0 = (accb + g) * 512 + sub * 65
                                mm(psum[:, c0:c0 + 65], PT[:, pbuf, g * 512 + sub * 128:g * 512 + (sub + 1) * 128],
                                   Vaug[:, kt, g, :], kt == 0 and sub == 0, kt == nkt - 1, ["PT%d" % pbuf, "Vaug", "Vones"],
                                   bkres(accb + g), skip=True)
                    for g in range(2):
                        h = pair + 4 * g
                        acc3 = psum[:, (accb + g) * 512:(accb + g) * 512 + 260].rearrange("p (s c) -> p s c", s=4)
                        rr = sm[:, 80 + 4 * g:84 + 4 * g]
                        recip(rr, acc3[:, :, 64], bkres(accb + g), ["rr%d" % g])
                        tmp3 = sqt[:, g * 256:(g + 1) * 256].rearrange("p (s c) -> p s c", s=4)
                        tt("dve", tmp3, acc3[:, :, 0:64], bc(rr, 2, 64), ALU.mult, bkres(accb + g) + ["rr%d" % g],
                           ["atmp%d" % g])
                        tt("pool", mixed[:, qb * 4:(qb + 1) * 4, h * 64:(h + 1) * 64], tmp3,
                           ga[:, qb * 4:(qb + 1) * 4, h * 64:(h + 1) * 64], ALU.mult,
                           ["atmp%d" % g, "WB_done"] + [x for u in range(4) for x in KR(qb * 4 + u)],
                           ["mixA%d_%d" % (qb * 4 + u, h) for u in range(4)])
            dump("mixed2", mixed.rearrange("p t c -> p (t c)"),
                 ["mixA%d_%d" % (t, h) for t in range(nown) for h in range(8)] + ["mixR%d" % t for t in range(nown)])

            dma("sp", g_bc[:], fnw, "fnw", writes=["fnw_s"] + GB)
            for k in range(8):
                s_ = next_slot()
                dma("sp", stage[:, s_, :], w_out[k * 128:(k + 1) * 128, :], "st%d" % s_, writes=["st%d" % s_])
                tt(("dve", "pool")[k % 2], wo[:, k, :], stage[:, s_, :], gate_bc[:], ALU.mult, ["st%d" % s_, "WA_done"] + GATE, ["wo"])
            mixres = lambda t: ["mixA%d_%d" % (t, h) for h in range(8)] + ["mixR%d" % t]
            for t in range(nown):
                mb = t % 4
                tbk, ybk = {nown - 3: (0, 1), nown - 2: (2, 3), nown - 1: (4, 5)}.get(t, (6, 7))
                pb = bankbf(tbk)
                P.slack = {"pe": float(os.environ.get("K_SLACK_PE", "1.5"))}
                for k in range(8):
                    tr(pb[:, k * 128:(k + 1) * 128], mixed[:, t, k * 128:(k + 1) * 128], mixres(t), ["bk%d" % tbk])
                P.slack = {}
                cp("dve", mT[:, mb, :, :].rearrange("p k c -> p (k c)"), pb, ["bk%d" % tbk, "WA_done"], ["mT%d" % mb])
                s_ = next_slot()
                st = "st%d" % s_
                dma("sp", stage[:, s_, :], xo[t * 128:(t + 1) * 128, :], st, writes=[st])
                for c in range(2):
                    for k in range(8):
                        mm(bank(ybk), mT[:, mb, k, :], wo[:, k, c * 512:(c + 1) * 512], k == 0, k == 7,
                           ["mT%d" % mb, "wo"], ["bk%d" % ybk])
                    tt("dve", stage[:, s_, c * 512:(c + 1) * 512], bank(ybk), stage[:, s_, c * 512:(c + 1) * 512], ALU.add,
                       ["bk%d" % ybk, st], [st])
                ssq = sm[:, 100 + mb:101 + mb]
                P.add("dve", lambda e, s_=s_, ssq=ssq: e.scalar_tensor_tensor(
                    out=krf_bf, in0=stage[:, s_, :], scalar=1.0, in1=stage[:, s_, :], op0=ALU.mult, op1=ALU.mult,
                    accum_out=ssq), reads=[st], writes=["fss%d" % mb, "qf_s", "qb_s"], cost=1.4)
                P.slack = {"act": float(os.environ.get("K_SLACK_ACT", "0.5"))}
                rsqrt_chain(ssq, ssq, 1.0 / D, ["fss%d" % mb], "fss%d" % mb)
                P.slack = {}
                stt(stage[:, s_, :], stage[:, s_, :], ssq, g_bc[:], ALU.mult, ALU.mult, [st, "fss%d" % mb, "fnw_s"], [st])
                dma("sp", y[t * 128:(t + 1) * 128, :], stage[:, s_, :], "out%d" % s_, reads=[st], writes=["y%d" % t])

        P.slack = {}
        fin_reads = ["dbg_" + n for n in dbg_out] + ["y%d" % t for t in range(NOWN)]
        P.add("sp", lambda e: e.nop(), reads=fin_reads, name="final")
        P.resolve()
        with nc.Block() as block:
            P.emit(block, sems, dsem)
    return nc


def _rope_tables(n_rows, head_dim):
    rows, cols = np.meshgrid(np.arange(n_rows, dtype=np.float32), np.arange(64, dtype=np.float32), indexing="ij")
    rows = rows.reshape(-1)
    cols = cols.reshape(-1)
    n_axis = head_dim // 4
    inv_freq = (np.float32(10000.0) ** (-np.arange(n_axis, dtype=np.float32) / np.float32(n_axis))).astype(np.float32)
    ang = np.concatenate([rows[:, None] * inv_freq, cols[:, None] * inv_freq], axis=-1).astype(np.float32)
    return np.cos(ang).astype(np.float32), np.sin(ang).astype(np.float32)


def _host_inputs(x, c, ctx, c_ctx, norm_w, w_mod, b_mod, w_in, q_norm_w, k_norm_w, ret_decay_fwd, ret_decay_bwd,
                 ret_gn_w, w_out, final_norm_w):
    f32 = np.float32
    x = np.asarray(x, f32)
    ctx = np.asarray(ctx, f32)
    w_in0 = np.asarray(w_in, f32)[0]
    qa = np.arange(0, 512).reshape(8, 64)
    qa_perm = np.concatenate([np.concatenate([qa[i], qa[i + 4]]) for i in range(4)])
    cols = np.concatenate([qa_perm, np.arange(768, 1280), np.arange(2816, 3328),
                           np.arange(512, 640), np.arange(640, 768), np.arange(1280, 1792),
                           np.arange(1792, 2304), np.arange(2304, 2816)])
    w_in_p = np.ascontiguousarray(w_in0[:, cols])
    cos_a, sin_a = _rope_tables(SEQ // 64, 64)
    cos_r, sin_r = _rope_tables(SEQ // 64, 128)
    ksc = f32(128.0 ** -0.5)
    tab_lat = np.concatenate([cos_a, sin_a, cos_r, sin_r, cos_r * ksc, sin_r * ksc], axis=1).astype(f32)
    tab_ctx = np.zeros((CTX, 320), f32)
    tab_ctx[:, 0:32] = 1.0
    tab_ctx[:, 64:128] = 1.0
    tab_ctx[:, 192:256] = ksc
    p = np.arange(128, dtype=f32)
    ii = np.arange(128, dtype=f32)
    cst = np.zeros((128, 898), f32)
    cst[:, 0] = 127 - p
    cst[:, 1] = p
    cst[:, 2:130] = (ii + 1)[None, :]
    cst[:, 130:258] = (128 - ii)[None, :]
    dif = ii[None, :] - p[:, None]
    cst[:, 258:386] = np.maximum(dif, 0)
    cst[:, 386:514] = np.maximum(-dif, 0)
    cst[:, 514:642] = (dif > 0)
    cst[:, 642:770] = (dif < 0)
    cst[:, 770:898] = 2.0 * (dif == 0)
    sel = np.zeros((2, 256), f32)
    sel[0, 0:128] = 1.0
    sel[1, 128:256] = 1.0
    ident = np.eye(128, dtype=f32)
    common = dict(
        w_mod=np.ascontiguousarray(np.asarray(w_mod, f32)[0]),
        b_mod2=np.ascontiguousarray(np.broadcast_to(np.asarray(b_mod, f32)[0][None, :], (2, 3 * D))),
        norm_w2=np.ascontiguousarray(np.broadcast_to(np.asarray(norm_w, f32)[0][None, :], (2, D))),
        w_in=w_in_p,
        qkn=np.ascontiguousarray(np.broadcast_to(
            np.concatenate([np.asarray(q_norm_w, f32)[0], np.asarray(k_norm_w, f32)[0]])[None, :], (128, 128))),
        dec=np.ascontiguousarray(np.broadcast_to(
            np.concatenate([np.asarray(ret_decay_fwd, f32)[0], np.asarray(ret_decay_bwd, f32)[0]])[None, :], (128, 8))),
        gnw=np.ascontiguousarray(np.broadcast_to(np.asarray(ret_gn_w, f32)[0][None, :], (128, 512))),
        fnw=np.ascontiguousarray(np.broadcast_to(np.asarray(final_norm_w, f32)[None, :], (128, D))),
        w_out=np.ascontiguousarray(np.asarray(w_out, f32)[0]),
        ident=ident, cst=cst, sel=sel,
    )
    in_maps = []
    c = np.asarray(c, f32)
    c_ctx = np.asarray(c_ctx, f32)
    for core in range(8):
        b, s = divmod(core, 4)
        t0, t1 = s * 2048, (s + 1) * 2048
        own = np.arange(t0, t1)
        oth = np.concatenate([np.arange(0, t0), np.arange(t1, SEQ)])
        xo = np.concatenate([x[b, own], x[b, oth], ctx[b]], axis=0)
        tabc = np.concatenate([tab_lat[own], tab_lat[oth], tab_ctx], axis=0).reshape(NT, 128, 320)
        df = np.where(oth < t0, t0 - 1 - oth, BIGD).astype(f32)
        db = np.where(oth >= t1, oth - t1, BIGD).astype(f32)
        cc = np.arange(CTX, dtype=f32)
        df = np.concatenate([df, t0 + 255 - cc])
        db = np.concatenate([db, SEQ + cc - t1])
        dist = np.stack([df.reshape(50, 128).T, db.reshape(50, 128).T], axis=-1).reshape(128, 100).astype(f32)
        cvv = np.stack([c[b].reshape(8, 128).T, c_ctx.reshape(8, 128).T], axis=-1).reshape(128, 16).astype(f32)
        m = dict(common)
        m.update(xo=np.ascontiguousarray(xo), tab=np.ascontiguousarray(tabc), dist=np.ascontiguousarray(dist),
                 cv=np.ascontiguousarray(cvv))
        in_maps.append(m)
    return in_maps


_NC_CACHE = {}


def kernel(x, c, ctx, c_ctx, norm_w, w_mod, b_mod, w_in, q_norm_w, k_norm_w, ret_decay_fwd, ret_decay_bwd,
           ret_gn_w, w_out, final_norm_w):
    in_maps = _host_inputs(x, c, ctx, c_ctx, norm_w, w_mod, b_mod, w_in, q_norm_w, k_norm_w, ret_decay_fwd,
                           ret_decay_bwd, ret_gn_w, w_out, final_norm_w)
    if "nc" not in _NC_CACHE:
        _NC_CACHE["nc"] = build_program()
    nc = _NC_CACHE["nc"]
    res = run_bass_kernel_spmd(nc, in_maps, core_ids=list(range(8)))
    out = np.zeros((2, SEQ, D), np.float32)
    for core in range(8):
        b, s = divmod(core, 4)
        out[b, s * 2048:(s + 1) * 2048] = res.results[core]["y"]
    return out
```

```python
import os
from contextlib import ExitStack

import numpy as np
import concourse.bass as bass
import concourse.mybir as mybir
from concourse.bass_utils import run_bass_kernel_spmd

F32 = mybir.dt.float32
BF16 = mybir.dt.bfloat16
AF = mybir.ActivationFunctionType
ALU = mybir.AluOpType
AX = mybir.AxisListType

D = 1024
SEQ = 8192
CTX = 256
NT = 66
NOWN = 16
EPS = 1e-6
BIGD = 1.0e6
ENGS = ("sp", "act", "dve", "pool", "pe")

WA_COLS = 1536
WB_COLS = 1792
STRICT_SAME_ENGINE = os.environ.get("K_STRICT", "1") == "1"


class Op:
    __slots__ = ("eng", "fn", "deps", "pos", "need_inc", "ordinal", "is_dma",
                 "sem_key", "cum", "waits", "known", "name", "all_deps", "cost", "lat", "idx", "fin", "nsucc", "succ", "slack")


class Prog:
    XLAT = float(os.environ.get("K_XLAT", "0.1"))

    def __init__(self):
        self.ops = {e: [] for e in ENGS}
        self.all = []
        self.segments = [[]]
        self.last_writer = {}
        self.readers = {}
        self.dma_counts = {}
        self.group_keys = set()
        self.slack = {}

    def add(self, eng, fn, reads=(), writes=(), dma_key=None, name="", cost=0.1, lat=0.0):
        op = Op()
        op.eng = eng
        op.fn = fn
        op.name = name
        op.is_dma = dma_key is not None
        op.sem_key = dma_key
        op.need_inc = False
        op.ordinal = 0
        op.waits = []
        op.known = None
        op.cum = 0
        op.cost = cost
        op.lat = lat
        op.idx = len(self.all)
        op.slack = self.slack.get(eng, 0.0)
        bkr = [r for r in reads if isinstance(r, str) and r.startswith("bk")]
        if bkr:
            writes = list(writes) + [r for r in bkr if r not in writes]
        deps = {}
        for r in reads:
            w = self.last_writer.get(r)
            if w is not None:
                deps[id(w)] = (w, "raw")
        for wr in writes:
            w = self.last_writer.get(wr)
            if w is not None and id(w) not in deps:
                deps[id(w)] = (w, "waw")
            for rd in self.readers.get(wr, ()):
                if id(rd) not in deps:
                    deps[id(rd)] = (rd, "war")
        op.all_deps = [(w, kind) for (w, kind) in deps.values() if w is not op]
        for r in reads:
            self.readers.setdefault(r, []).append(op)
        for wr in writes:
            self.last_writer[wr] = op
            self.readers[wr] = []
        self.all.append(op)
        self.segments[-1].append(op)
        return op

    def barrier(self):
        self.segments.append([])

    def _schedule_segment(self, seg, t0):
        inseg = set(id(o) for o in seg)
        for o in seg:
            o.succ = []
        npred = {}
        for o in seg:
            n = 0
            for d, _k in o.all_deps:
                if id(d) in inseg:
                    d.succ.append(o)
                    n += 1
            npred[id(o)] = n
        blv = {}
        for o in reversed(seg):
            m = 0.0
            for sct in o.succ:
                v = blv[id(sct)]
                if v > m:
                    m = v
            blv[id(o)] = m + o.cost + o.lat
        use_bl = os.environ.get("K_PRIO", "bl") == "bl"
        ready = {e: [] for e in ENGS}
        rtime = {}
        for o in seg:
            if npred[id(o)] == 0:
                ready[o.eng].append(o)
                rtime[id(o)] = t0
        free = {e: t0 for e in ENGS}
        out = []
        remaining = len(seg)
        tend = t0
        while remaining:
            best = None
            for e in ENGS:
                if not ready[e]:
                    continue
                fe = free[e]
                cand = None
                for o in ready[e]:
                    st = rtime[id(o)]
                    if st < fe:
                        st = fe
                    if use_bl:
                        key = (st, -blv[id(o)], o.idx)
                    else:
                        key = (st, o.idx)
                    if cand is None or key < cand[0]:
                        cand = (key, o)
                if best is None or cand[0] < best[0]:
                    best = cand
            o = best[1]
            st = best[0][0]
            ready[o.eng].remove(o)
            free[o.eng] = st + o.cost
            o.fin = st + o.cost + o.lat
            if o.fin > tend:
                tend = o.fin
            out.append(o)
            remaining -= 1
            for sct in o.succ:
                k = id(sct)
                npred[k] -= 1
                t = o.fin + (self.XLAT if (sct.eng != o.eng or o.is_dma) else 0.0)
                t += sct.slack
                if rtime.get(k, t0) < t:
                    rtime[k] = t
                if npred[k] == 0:
                    ready[sct.eng].append(sct)
        return out, tend

    def resolve(self):
        order = []
        t = 0.0
        nseg = len(self.segments)
        for si, seg in enumerate(self.segments):
            tprev = t
            sched, t = self._schedule_segment(seg, t)
            order += sched
            if os.environ.get("K_VERBOSE"):
                busy = {e: 0.0 for e in ENGS}
                for o in sched:
                    busy[o.eng] += o.cost
                print("segment %d: %.1f us (%.1f -> %.1f) busy %s" % (si, t - tprev, tprev, t, {e: round(v, 1) for e, v in busy.items()}))
            if si < nseg - 1:
                lasts = {}
                dm = {}
                for o in sched:
                    if o.is_dma:
                        dm[o.sem_key] = o
                    else:
                        lasts[o.eng] = o
                bl = list(lasts.values()) + list(dm.values())
                for e in ENGS:
                    b = Op()
                    b.eng = e
                    b.fn = lambda eng: eng.nop()
                    b.name = "barrier"
                    b.is_dma = False
                    b.sem_key = None
                    b.need_inc = False
                    b.ordinal = 0
                    b.waits = []
                    b.known = None
                    b.cum = 0
                    b.cost = 0.05
                    b.lat = 0
                    b.idx = -1
                    b.slack = 0.0
                    b.all_deps = [(o, "raw") for o in bl]
                    order.append(b)
                t += 1.0
        self.sched_time = t
        self.dma_counts = {}
        self.ops = {e: [] for e in ENGS}
        for o in order:
            if o.is_dma:
                c = self.dma_counts.get(o.sem_key, 0) + 1
                self.dma_counts[o.sem_key] = c
                o.cum = c
            o.pos = len(self.ops[o.eng])
            self.ops[o.eng].append(o)
        self.all = order
        for o in order:
            o.deps = []
            for w, kind in o.all_deps:
                if (not w.is_dma) and (not o.is_dma) and w.eng == o.eng:
                    if o.eng == "pe" or (kind != "raw" and not STRICT_SAME_ENGINE):
                        continue
                o.deps.append(w)
        prev_known = {e: {} for e in ENGS}
        for op in order:
            known = dict(prev_known[op.eng])
            items = []
            for d in op.deps:
                if d.is_dma:
                    val = self.dma_counts[d.sem_key] if d.sem_key in self.group_keys else d.cum
                    items.append((("dma", d.sem_key), val, d))
                else:
                    items.append((d.eng, d.pos + 1, d))
            items.sort(key=lambda x: -x[1])
            for ck, val, d in items:
                if known.get(ck, 0) >= val:
                    continue
                op.waits.append((d, val))
                d.need_inc = True
                known[ck] = val
                if d.known:
                    for k, v in d.known.items():
                        if known.get(k, 0) < v:
                            known[k] = v
            op.known = known
            prev_known[op.eng] = known
        for e in ENGS:
            n = 0
            for op in self.ops[e]:
                if op.is_dma:
                    continue
                if op.need_inc:
                    n += 1
                    op.ordinal = n

    def emit(self, block, sems, dma_sems):
        prog = self

        def run(engname):
            def body(eng):
                for op in prog.ops[engname]:
                    for d, val in op.waits:
                        if d.is_dma:
                            eng.wait_ge(dma_sems[d.sem_key], 16 * val)
                        else:
                            eng.wait_ge(sems[d.eng], d.ordinal)
                    ins = op.fn(eng)
                    if op.is_dma:
                        ins.then_inc(dma_sems[op.sem_key], 16)
                    elif op.need_inc:
                        ins.then_inc(sems[engname], 1)
            return body

        block.sync(run("sp"))
        block.scalar(run("act"))
        block.vector(run("dve"))
        block.gpsimd(run("pool"))
        block.tensor(run("pe"))


def bc(ap, axis, n):
    u = ap.unsqueeze(axis)
    shp = list(u.shape)
    shp[axis] = n
    return u.to_broadcast(shp)


def build_program(debug=None):
    debug = debug or {}
    nc = bass.Bass("TRN2", target_bir_lowering=False)

    def din(name, shape):
        return nc.dram_tensor(name, list(shape), F32, kind="ExternalInput").ap()

    xo = din("xo", [NT * 128, D])
    tab = din("tab", [NT, 128, 320])
    dist = din("dist", [128, 100])
    cv = din("cv", [128, 16])
    w_mod = din("w_mod", [D, 3 * D])
    b_mod2 = din("b_mod2", [2, 3 * D])
    norm_w2 = din("norm_w2", [2, D])
    w_in = din("w_in", [D, 3328])
    qkn = din("qkn", [128, 128])
    dec = din("dec", [128, 8])
    gnw = din("gnw", [128, 512])
    fnw = din("fnw", [128, D])
    w_out = din("w_out", [D, D])
    ident = din("ident", [128, 128])
    cst = din("cst", [128, 898])
    sel = din("sel", [2, 256])
    y = nc.dram_tensor("y", [NOWN * 128, D], F32, kind="ExternalOutput").ap()
    dbg_out = {}
    for name, shape in debug.items():
        dbg_out[name] = nc.dram_tensor("dbg_" + name, list(shape), F32, kind="ExternalOutput").ap()

    P = Prog()
    P.group_keys.add("const")

    with ExitStack() as es:
        def sb(name, shape, dt=F32):
            return es.enter_context(nc.sbuf_tensor(name, list(shape), dt))

        identf = sb("identf", [128, 128])
        identb = sb("identb", [128, 128], BF16)
        kT = sb("kT", [128, NT * 128], BF16)
        Vaug = sb("Vaug", [128, NT, 2, 65], BF16)
        gate_bc = sb("gate_bc", [128, D])
        g_bc = sb("g_bc", [128, D])
        shift_bc = sb("shift_bc", [128, D])
        arenaA = sb("arenaA", [128, 8 * WA_COLS], BF16)
        arenaB = sb("arenaB", [128, 16 * 1024], BF16)
        arenaR = sb("arenaR", [128, 24576], BF16)
        stage = sb("stage", [128, 4, D])
        hb = sb("hb", [128, 2, D], BF16)
        xT = sb("xT", [128, 2, 8, 128], BF16)
        tabs = sb("tabs", [128, 2, 320])
        junk = sb("junk", [128, D], BF16)
        sm = sb("sm", [128, 128])
        qkn_s = sb("qkn_s", [128, 128])
        dec_s = sb("dec_s", [128, 8])
        lg = sb("lg", [128, 8])
        cvs = sb("cvs", [128, 16])
        scv = sb("scv", [128, 16])
        cvt = sb("cvt", [128, 16])
        sel_s = sb("sel_s", [2, 256])
        dist_s = sb("dist_s", [128, 100])
        wts = sb("wts", [128, 50, 2, 4])
        DT = sb("DT", [128, 4, 128])
        QDF = sb("QDF", [128, 4, 128], BF16)
        QDB = sb("QDB", [128, 4, 128], BF16)
        kdec = sb("kdec", [128, 8])
        gnw_s = sb("gnw_s", [128, 512])
        Sf0 = sb("Sf0", [128, 512])
        Sb0 = sb("Sb0", [128, 512])
        sqt = sb("sqt", [128, 512])
        kn = sb("kn", [128, 128])
        m1a = sb("m1a", [128, 128])
        m2a = sb("m2a", [128, 128])
        kro = sb("kro", [128, 128], BF16)
        m1r = sb("m1r", [128, 512])
        m2r = sb("m2r", [128, 512])
        krf = sb("krf", [128, 512])
        kwf = sb("kwf", [128, 512], BF16)
        kwb = sb("kwb", [128, 512], BF16)
        vrb = sb("vrb", [128, 512], BF16)
        b512 = sb("b512", [128, 512], BF16)
        q512 = sb("q512", [128, 512], BF16)
        dummy = sb("mk_dummy", [128, 8])

        psum = es.enter_context(nc.psum_tensor("psum", [128, 8 * 512], F32))

        def bank(i, n=1):
            return psum[:, i * 512:(i + n) * 512]

        def bankbf(i):
            return psum[:, i * 512:(i + 1) * 512].bitcast(BF16)

        kT32 = kT[:, :].bitcast(F32)
        xslots = [kT32[:, i * 1024:(i + 1) * 1024] for i in range(4)]
        WK = ["wk%d" % i for i in range(4)]
        W_A = arenaA[:, :].rearrange("p (k c) -> p k c", k=8)
        W_B = arenaB[:, 0:8 * WB_COLS].rearrange("p (k c) -> p k c", k=8)
        mixed = arenaB[:, :].rearrange("p (t c) -> p t c", t=16)
        wo = arenaA[:, 0:8192].rearrange("p (k c) -> p k c", k=8)
        mT = arenaA[:, 8192:8192 + 4096].rearrange("p (b k c) -> p b k c", b=4, k=8)
        qrT = arenaR[:, 0:8192].rearrange("p (h t) -> p h t", h=4)
        kr = arenaR[:, 8192:16384].rearrange("p (t c) -> p t c", t=16)
        vr = arenaR[:, 16384:24576].rearrange("p (t c) -> p t c", t=16)
        qT = arenaR[:, 0:8192].rearrange("p (h t) -> p h t", h=4)
        ga = arenaR[:, 8192:16384].rearrange("p (t c) -> p t c", t=16)
        NPT = int(os.environ.get("K_NPT", "4"))
        PT = arenaR[:, 16384:16384 + 1024 * NPT].rearrange("p (b c) -> p b c", b=NPT)
        R32 = arenaR[:, :].bitcast(F32)
        modrows = R32[0:2, 0:3072]
        bm2 = R32[0:2, 3072:6144]
        grow = R32[0:2, 6144:7168]
        nw2 = R32[0:2, 7168:8192]
        g_bc_c = R32[:, 8192:9216]
        shift_bc_c = R32[:, 9216:10240]
        cst_s = R32[:, 10240:10240 + 898]
        tmp1 = R32[0:2, 11264:12288]
        SbN = stage[:, :, :].bitcast(BF16).rearrange("p s (a c) -> p (s a) c", a=4)

        sems = {e: es.enter_context(nc.semaphore("s_" + e)) for e in ENGS}
        dma_keys = ["const", "st0", "st1", "st2", "st3", "tab0", "tab1", "out0", "out1", "out2", "out3", "dbg", "fnw"] + ["wk%d" % i for i in range(8)]
        dsem = {k: es.enter_context(nc.semaphore("d_" + k)) for k in dma_keys}

        def nfree(ap):
            n = 1
            for d in ap.shape[1:]:
                n *= d
            return n

        def ecost(eng, ap, f32=True):
            n = nfree(ap)
            if eng == "act":
                return (n + 150) / 1200.0
            if eng == "dve":
                return n / 900.0 + 0.1
            if eng == "pool":
                return n * 1.8 / 1000.0 + 0.3
            return 0.1

        def dma(eng, out, in_, key, reads=(), writes=(), name=""):
            nbytes = nfree(out) * out.shape[0] * (4 if out.dtype == F32 else 2)
            return P.add(eng, lambda e, o=out, i=in_: e.dma_start(out=o, in_=i), reads=reads, writes=writes,
                         dma_key=key, name=name, cost=0.08, lat=2.0 + nbytes / 1.5e5)

        def act(out, in_, func, reads, writes, scale=1.0, bias=0.0, accum=None, name=""):
            def fn(e, out=out, in_=in_, func=func, scale=scale, bias=bias, accum=accum):
                kw = {}
                if accum is not None:
                    kw["accum_out"] = accum
                return e.activation(out=out, in_=in_, func=func, bias=bias, scale=scale, **kw)
            return P.add("act", fn, reads=reads, writes=writes, name=name, cost=ecost("act", in_) + (0.1 if accum is not None else 0))

        def tt(eng, out, in0, in1, op, reads, writes, name=""):
            return P.add(eng, lambda e, o=out, a=in0, b=in1, op=op: e.tensor_tensor(out=o, in0=a, in1=b, op=op),
                         reads=reads, writes=writes, name=name, cost=ecost(eng, out))

        def ts(eng, out, in0, s1, s2, op0, op1, reads, writes, name=""):
            def fn(e, out=out, in0=in0, s1=s1, s2=s2, op0=op0, op1=op1):
                if op1 is None:
                    return e.tensor_scalar(out=out, in0=in0, scalar1=s1, scalar2=None, op0=op0)
                return e.tensor_scalar(out=out, in0=in0, scalar1=s1, scalar2=s2, op0=op0, op1=op1)
            return P.add(eng, fn, reads=reads, writes=writes, name=name, cost=ecost(eng, out))

        def stt(out, in0, scalar, in1, op0, op1, reads, writes, name=""):
            return P.add("dve", lambda e, o=out, a=in0, s=scalar, b=in1, op0=op0, op1=op1:
                         e.scalar_tensor_tensor(out=o, in0=a, scalar=s, in1=b, op0=op0, op1=op1),
                         reads=reads, writes=writes, name=name, cost=ecost("dve", out) * 1.2)

        def cp(eng, out, in_, reads, writes, name=""):
            if eng == "act":
                return P.add("act", lambda e, o=out, i=in_: e.copy(out=o, in_=i), reads=reads, writes=writes, name=name,
                             cost=ecost("act", out))
            return P.add(eng, lambda e, o=out, i=in_: e.tensor_copy(out=o, in_=i), reads=reads, writes=writes, name=name,
                         cost=ecost(eng, out))

        def mm(out, lhsT, rhs, start, stop, reads, writes, name="", skip=False):
            def fn(e, out=out, lhsT=lhsT, rhs=rhs, start=start, stop=stop, skip=skip):
                if skip:
                    return e.matmul(out, lhsT=lhsT, rhs=rhs, start=start, stop=stop, skip_group_check=True)
                return e.matmul(out, lhsT=lhsT, rhs=rhs, start=start, stop=stop)
            n = max(nfree(rhs), 64)
            c = n / 2400.0 + 0.025
            if lhsT.shape[0] <= 64:
                c *= 0.65
            if rhs.dtype == F32:
                c *= 4
            return P.add("pe", fn, reads=reads, writes=writes, name=name, cost=c)

        def tr(out, in_, reads, writes, name=""):
            return P.add("pe", lambda e, o=out, i=in_: e.transpose(out=o, in_=i, identity=identb[:]),
                         reads=list(reads) + ["identb"], writes=writes, name=name, cost=0.09)

        def recip(out, in_, reads, writes):
            return P.add("dve", lambda e, o=out, i=in_: e.reciprocal(out=o, in_=i), reads=reads, writes=writes,
                         cost=ecost("dve", out))

        def rsqrt_chain(out, in_, scale, reads_in, key):
            act(out, in_, AF.Ln, reads=reads_in, writes=[key], scale=scale, bias=EPS)
            act(out, out, AF.Exp, reads=[key], writes=[key], scale=-0.5)

        def dump(name, ap_src, reads):
            if name not in dbg_out:
                return
            dma("pool" if ap_src.dtype != F32 else "sp", dbg_out[name], ap_src, "dbg", reads=reads, writes=["dbg_" + name])

        for (dst, src, res) in [
            (cvs[:], cv, "cvs"), (bm2, b_mod2, "bm2"), (nw2, norm_w2, "nw2"), (qkn_s[:], qkn, "qkn"),
            (dec_s[:], dec, "dec"), (cst_s, cst, "cst"), (sel_s[:], sel, "sel"), (identf[:], ident, "identf"),
            (dist_s[:], dist, "dist"), (gnw_s[:], gnw, "gnw"),
        ]:
            dma("sp", dst, src, "const", writes=[res])

        cp("dve", identb[:], identf[:], ["identf"], ["identb"])

        act(lg[:], dec_s[:], AF.Exp, ["dec"], ["lg"], scale=-1.0)
        act(lg[:], lg[:], AF.Ln, ["lg"], ["lg"], scale=1.0, bias=1.0)
        ts("dve", lg[:], lg[:], -1.0, None, ALU.mult, None, ["lg"], ["lg"])
        c127 = cst_s[:, 0:1]
        cpp = cst_s[:, 1:2]
        act(kdec[:, 0:4], lg[:, 0:4], AF.Exp, ["lg", "cst"], ["kdecf"], scale=c127)
        act(kdec[:, 4:8], lg[:, 4:8], AF.Exp, ["lg", "cst"], ["kdecb"], scale=cpp)
        act(sm[:, 0:8], lg[:, 0:8], AF.Exp, ["lg"], ["cd8"], scale=128.0)
        for h in range(4):
            act(QDF[:, h, :], cst_s[:, 2:130], AF.Exp, ["lg", "cst"], ["QDF%d" % h], scale=lg[:, h:h + 1])
            act(QDB[:, h, :], cst_s[:, 130:258], AF.Exp, ["lg", "cst"], ["QDB%d" % h], scale=lg[:, 4 + h:5 + h])
            act(sqt[:, 0:128], cst_s[:, 258:386], AF.Exp, ["lg", "cst"], ["sqt"], scale=lg[:, h:h + 1])
            act(sqt[:, 128:256], cst_s[:, 386:514], AF.Exp, ["lg", "cst"], ["sqt"], scale=lg[:, 4 + h:5 + h])
            tt("dve", sqt[:, 0:256], sqt[:, 0:256], cst_s[:, 514:770], ALU.mult, ["sqt", "cst"], ["sqt"])
            tt("dve", sqt[:, 0:128], sqt[:, 0:128], sqt[:, 128:256], ALU.add, ["sqt"], ["sqt"])
            tt("dve", DT[:, h, :], sqt[:, 0:128], cst_s[:, 770:898], ALU.add, ["sqt", "cst"], ["DT%d" % h])
        dist3 = dist_s[:].rearrange("p (j d) -> p j d", d=2)
        for d_ in range(2):
            for h in range(4):
                ts("dve", wts[:, :, d_, h], dist3[:, :, d_], lg[:, d_ * 4 + h:d_ * 4 + h + 1], None, ALU.mult, None,
                   ["dist", "lg"], ["wts_%d%d" % (d_, h)])
        wts_keys = ["wts_%d%d" % (d_, h) for d_ in range(2) for h in range(4)]
        wflat = wts[:].rearrange("p j d h -> p (j d h)")
        act(wflat, wflat, AF.Exp, wts_keys, ["wts"])

        act(cvt[:], cvs[:], AF.Exp, ["cvs"], ["cvt"], scale=-1.0)
        ts("dve", cvt[:], cvt[:], 1.0, None, ALU.add, None, ["cvt"], ["cvt"])
        recip(cvt[:], cvt[:], ["cvt"], ["cvt"])
        tt("dve", scv[:], cvs[:], cvt[:], ALU.mult, ["cvs", "cvt"], ["scv"])
        scv3 = scv[:].rearrange("p (k w) -> p k w", w=2)

        slot_ctr = [0]
        wctr = [0]

        def wslot():
            j = wctr[0] % 8
            wctr[0] += 1
            if j < 4:
                return stage[:, j, :], "st%d" % j
            return xslots[j - 4], "wk%d" % (j - 4)

        def next_slot():
            s = slot_ctr[0] % 4
            slot_ctr[0] += 1
            return s

        for k in range(8):
            for j in range(3):
                sap, sres = wslot()
                dma("sp", sap, w_mod[k * 128:(k + 1) * 128, j * 1024:(j + 1) * 1024], sres, writes=[sres])
                for i in range(2):
                    bnk = j * 2 + i
                    mm(psum[0:2, bnk * 512:(bnk + 1) * 512], scv3[:, k, :], sap[:, i * 512:(i + 1) * 512],
                       k == 0, k == 7, ["scv", sres], ["bk%d" % bnk])
        for bnk in range(6):
            tt("dve", modrows[:, bnk * 512:(bnk + 1) * 512], psum[0:2, bnk * 512:(bnk + 1) * 512],
               bm2[:, bnk * 512:(bnk + 1) * 512], ALU.add, ["bk%d" % bnk, "bm2"], ["modrows%d" % bnk])
        ts("dve", tmp1, modrows[:, 1024:2048], 1.0, None, ALU.add, None, ["modrows2", "modrows3"], ["tmp1"])
        tt("dve", grow, tmp1, nw2, ALU.mult, ["tmp1", "nw2"], ["grow"])
        bjobs = [
            (grow, 0, g_bc[:], ["grow"], "g_bc"),
            (grow, 1, g_bc_c, ["grow"], "g_bc_c"),
            (modrows[:, 0:1024], 0, shift_bc[:], ["modrows0", "modrows1"], "shift_bc"),
            (modrows[:, 0:1024], 1, shift_bc_c, ["modrows0", "modrows1"], "shift_bc_c"),
            (modrows[:, 2048:3072], 0, gate_bc[:], ["modrows4", "modrows5"], "gate_bc"),
        ]
        bi = 0
        for (row, which, dst, rds, res) in bjobs:
            for c in range(2):
                bnk = 6 + (bi % 2)
                bi += 1
                mm(bank(bnk), sel_s[0:2, which * 128:(which + 1) * 128], row[:, c * 512:(c + 1) * 512], True, True,
                   rds + ["sel"], ["bk%d" % bnk])
                cp("act" if bi % 2 else "dve", dst[:, c * 512:(c + 1) * 512], bank(bnk), ["bk%d" % bnk],
                   [res + str(c)])
        GB = ["g_bc0", "g_bc1"]
        GBC = ["g_bc_c0", "g_bc_c1"]
        SBK = ["shift_bc0", "shift_bc1"]
        SBC = ["shift_bc_c0", "shift_bc_c1"]
        GATE = ["gate_bc0", "gate_bc1"]

        pieces = [(0, 1024, "A", 0), (1024, 1536, "A", 1024), (1536, 2560, "B", 0), (2560, 3328, "B", 1024)]
        ci = 0
        for k in range(8):
            for (c0, c1, which, off) in pieces:
                sap, sres = wslot()
                n = c1 - c0
                dma("sp", sap[:, 0:n], w_in[k * 128:(k + 1) * 128, c0:c1], sres, writes=[sres])
                dstW = W_A if which == "A" else W_B
                eng = ("pool", "dve", "act")[ci % 3]
                ci += 1
                cp(eng, dstW[:, k, off:off + n], sap[:, 0:n], [sres], ["W%s" % which])
        P.add("pool", lambda e: e.memset(Vaug[:, :, :, 64:65], 1.0), writes=["Vones", "Vaug", "kT"] + WK, cost=0.3)

        xctr = [0]

        def xproc(t, ctx_tile):
            i = xctr[0] % 2
            xctr[0] += 1
            s = next_slot()
            st = "st%d" % s
            dma("sp", stage[:, s, :], xo[t * 128:(t + 1) * 128, :], st, writes=[st])
            dma("sp", tabs[:, i, :], tab[t], "tab%d" % i, writes=["tabs%d" % i])
            ssq = sm[:, 8 + i:9 + i]
            act(junk[:], stage[:, s, :], AF.Square, [st], ["ssq%d" % i], accum=ssq)
            rsqrt_chain(ssq, ssq, 1.0 / D, ["ssq%d" % i], "ssq%d" % i)
            gsrc, grd = (g_bc_c, GBC) if ctx_tile else (g_bc[:], GB)
            ssrc, srd = (shift_bc_c, SBC) if ctx_tile else (shift_bc[:], SBK)
            stt(stage[:, s, :], stage[:, s, :], ssq, gsrc, ALU.mult, ALU.mult, [st, "ssq%d" % i] + grd, [st])
            tt(os.environ.get("K_HB", "dve"), hb[:, i, :], stage[:, s, :], ssrc, ALU.add, [st] + srd, ["hb%d" % i])
            pb = bankbf(i)
            for k in range(8):
                tr(pb[:, k * 128:(k + 1) * 128], hb[:, i, k * 128:(k + 1) * 128], ["hb%d" % i], ["bk%d" % i])
            cp(os.environ.get("K_XTEV", "dve"), xT[:, i, :, :].rearrange("p k c -> p (k c)"), pb, ["bk%d" % i], ["xT%d" % i])
            return i

        def inproj(i, Wv, c0, n, bnk, wres):
            for k in range(8):
                mm(psum[:, bnk * 512:bnk * 512 + n], xT[:, i, k, :], Wv[:, k, c0:c0 + n], k == 0, k == 7,
                   ["xT%d" % i, wres], ["bk%d" % bnk])

        def rope(src, nh, hd, cosb, sinb, dst, tmpa, tmpb, rd_src, rd_tab, wr, ra, rb, eng_mul=("dve", "dve"),
                 eng_add=tuple(os.environ.get("K_ROPEADD", "dve,pool").split(","))):
            n = nh * hd
            s4 = src.rearrange("p (h two d) -> p h two d", h=nh, two=2)
            cb = bc(bc(cosb, 1, 2), 1, nh)
            sbb = bc(bc(sinb, 1, 2), 1, nh)
            a4 = tmpa[:, 0:n].rearrange("p (h two d) -> p h two d", h=nh, two=2)
            b4 = tmpb[:, 0:n].rearrange("p (h two d) -> p h two d", h=nh, two=2)
            d4 = dst.rearrange("p (h two d) -> p h two d", h=nh, two=2)
            tt(eng_mul[0], a4, s4, cb, ALU.mult, rd_src + rd_tab, ra)
            tt(eng_mul[1], b4, s4, sbb, ALU.mult, rd_src + rd_tab, rb)
            tt(eng_add[0], d4[:, :, 0, :], a4[:, :, 0, :], b4[:, :, 1, :], ALU.subtract, ra + rb, [wr + "_lo"])
            tt(eng_add[1], d4[:, :, 1, :], a4[:, :, 1, :], b4[:, :, 0, :], ALU.add, ra + rb, [wr + "_hi"])

        first_state = [True]
        ON4 = ["on0", "on1", "on2", "on3"]

        def kv_epilogue(t, i, bnk):
            tb = "tabs%d" % i
            bk = "bk%d" % bnk
            ka = psum[:, bnk * 512:bnk * 512 + 128]
            va = psum[:, bnk * 512 + 128:bnk * 512 + 256]
            act(sqt[:, 0:128], ka, AF.Square, [bk], ["sqt"])
            P.add("dve", lambda e: e.tensor_reduce(out=sm[:, 16:18], in_=sqt[:, 0:128].rearrange("p (h d) -> p h d", h=2),
                                                   axis=AX.X, op=ALU.add), reads=["sqt"], writes=["rk"])
            rsqrt_chain(sm[:, 16:18], sm[:, 16:18], 1.0 / 64, ["rk"], "rk")
            for h in range(2):
                stt(kn[:, h * 64:(h + 1) * 64], ka[:, h * 64:(h + 1) * 64], sm[:, 16 + h:17 + h], qkn_s[:, 64:128],
                    ALU.mult, ALU.mult, [bk, "rk", "qkn"], ["kn%d" % h])
            rope(kn[:, :], 2, 64, tabs[:, i, 0:32], tabs[:, i, 32:64], kro[:, :], m1a, m2a, ["kn0", "kn1"], [tb], "kro", ["m1a"], ["m2a"])
            pb = bankbf(5)
            tr(pb[:, 0:128], kro[:, :], ["kro_lo", "kro_hi"], ["bk5"])
            cp("act", kT[:, t * 128:(t + 1) * 128], pb[:, 0:128], ["bk5"], ["kT"])
            cp("act", Vaug[:, t, :, 0:64], va.rearrange("p (g d) -> p g d", g=2), [bk], ["Vaug"])

        def kr_rope(i, bnk, dst, wr):
            rope(psum[:, bnk * 512:(bnk + 1) * 512], 4, 128, tabs[:, i, 192:256], tabs[:, i, 256:320], dst, m1r, m2r,
                 ["bk%d" % bnk], ["tabs%d" % i], wr, ["m1r"], ON4)

        def other_tile(t, ctx_tile):
            j = t - NOWN
            i = xproc(t, ctx_tile)
            inproj(i, W_B, 0, 256, 2, "WB")
            kv_epilogue(t, i, 2)
            inproj(i, W_B, 768, 512, 3, "WB")
            kr_rope(i, 3, krf[:, :], "krf")
            inproj(i, W_B, 1280, 512, 4, "WB")
            cp("act", vrb[:, :], bank(4), ["bk4"], ["vrb"])
            k3 = krf[:, :].rearrange("p (h d) -> p h d", h=4)
            for h in range(4):
                hs = slice(h * 128, (h + 1) * 128)
                act(kwf[:, hs], krf[:, hs], AF.Copy, ["krf_lo", "krf_hi", "wts"], ["kwf"], scale=wts[:, j, 0, h:h + 1])
                act(kwb[:, hs], krf[:, hs], AF.Copy, ["krf_lo", "krf_hi", "wts"], ["kwb"], scale=wts[:, j, 1, h:h + 1])
            for (kw, res, bnk) in ((kwf, "kwf", 6), (kwb, "kwb", 7)):
                for h in range(4):
                    st_ = first_state[0] and h == 0
                    mm(psum[:, bnk * 512 + h * 128:bnk * 512 + (h + 1) * 128], kw[:, h * 128:(h + 1) * 128],
                       vrb[:, h * 128:(h + 1) * 128], st_, False, [res, "vrb"], ["bk%d" % bnk], skip=True)
            first_state[0] = False

        def own_tile_p1(t):
            i = xproc(t, False)
            inproj(i, W_B, 0, 256, 2, "WB")
            kv_epilogue(t, i, 2)
            inproj(i, W_B, 768, 512, 3, "WB")
            kr_rope(i, 3, kr[:, t, :], "kr%d" % t)
            inproj(i, W_B, 1280, 512, 4, "WB")
            cp("act", vr[:, t, :], bank(4), ["bk4"], ["vr%d" % t])
            inproj(i, W_B, 256, 512, 3, "WB")
            rope(bank(3), 4, 128, tabs[:, i, 64:128], tabs[:, i, 128:192], b512[:, :], m1r, m2r, ["bk3"],
                 ["tabs%d" % i], "b512", ["m1r"], ON4)
            pb = bankbf(5)
            for h in range(4):
                tr(pb[:, 512 + h * 128:512 + (h + 1) * 128], b512[:, h * 128:(h + 1) * 128], ["b512_lo", "b512_hi"],
                   ["bk5"])
            cp("act", qrT[:, :, t * 128:(t + 1) * 128], pb[:, 512:1024].rearrange("p (h c) -> p h c", h=4), ["bk5"],
               ["qrT%d" % t])

        order = [64, 65] + list(range(16, 64))
        lim = int(os.environ.get("K_NOTH", "50"))
        order = order[:lim]
        for t in order:
            other_tile(t, t >= 64)
        if order:
            cp("act", Sf0[:, :], bank(6), ["bk6"], ["Sf0"])
            cp("dve", Sb0[:, :], bank(7), ["bk7"], ["Sb0"])
        dump("kT", kT[:, :], ["kT"])
        dump("Vaug", Vaug[:].rearrange("p t g d -> p (t g d)"), ["Vaug"])
        dump("Sf0", Sf0[:, :], ["Sf0"])
        dump("Sb0", Sb0[:, :], ["Sb0"])
        dump("wts", wts[:].rearrange("p j d h -> p (j d h)"), ["wts"])
        dump("gbc", g_bc[:], GB)
        dump("sbc", shift_bc[:], SBK)
        dump("gbcc", g_bc_c, GBC)
        dump("sbcc", shift_bc_c, SBC)
        dump("gate", gate_bc[:], GATE)
        dump("DT", DT[:].rearrange("p h c -> p (h c)"), ["DT0", "DT1", "DT2", "DT3"])
        P.barrier()


        def bkres(n):
            return ["bk5", "bk5"] if n == 5 else ["bk%d" % n]

        nown = int(os.environ.get("K_NOWN", "16"))
        stop_after = os.environ.get("K_STOP", "")

        own_order = list(range(nown))
        if os.environ.get("K_OWNREV", "1") == "1":
            own_order.reverse()
        for t in own_order:
            own_tile_p1(t)
        KR = lambda t: ["kr%d_lo" % t, "kr%d_hi" % t]
        dump("qrT", qrT.rearrange("p h t -> p (h t)"), ["qrT%d" % t for t in range(nown)])
        dump("kr", kr.rearrange("p t c -> p (t c)"), [x for t in range(nown) for x in KR(t)])
        dump("vr", vr.rearrange("p t c -> p (t c)"), ["vr%d" % t for t in range(nown)])
        P.add("pool", lambda e: e.memset(dummy[:, 0:2], 0.0), reads=[], writes=["WB", "WB_done", "dummy0"], cost=0.1)

        krf_bf = krf[:, :].bitcast(BF16)
        qf_s = krf_bf[:, 0:512]
        qb_s = krf_bf[:, 512:1024]
        DTf = DT[:].rearrange("p h c -> p (h c)")
        if stop_after != "p1":
            for n in range(nown - 1, -1, -1):
                cp("act", SbN[:, n, :], Sb0[:, :], ["Sb0"], ["st%d" % (n // 4)])
                tt("pool", kwb[:, :].rearrange("p (h d) -> p h d", h=4), kr[:, n, :].rearrange("p (h d) -> p h d", h=4),
                   bc(kdec[:, 4:8], 2, 128), ALU.mult, KR(n) + ["kdecb"], ["kwb"])
                for h in range(4):
                    mm(psum[:, 6 * 512 + h * 128:6 * 512 + (h + 1) * 128], kwb[:, h * 128:(h + 1) * 128],
                       vr[:, n, h * 128:(h + 1) * 128], True, True, ["kwb", "vr%d" % n], ["bk6"])
                for h in range(4):
                    stt(Sb0[:, h * 128:(h + 1) * 128], Sb0[:, h * 128:(h + 1) * 128], sm[:, 4 + h:5 + h],
                        psum[:, 6 * 512 + h * 128:6 * 512 + (h + 1) * 128], ALU.mult, ALU.add, ["Sb0", "cd8", "bk6"], ["Sb0"])
            for n in range(nown):
                tsl = slice(n * 128, (n + 1) * 128)
                cp("act", b512[:, :], Sf0[:, :], ["Sf0"], ["b512_lo", "b512_hi"])
                pb = bankbf(5)
                for h in range(4):
                    tr(pb[:, h * 128:(h + 1) * 128], kr[:, n, h * 128:(h + 1) * 128], KR(n), ["bk5"])
                cp("act", vrb[:, :], pb[:, 0:512], ["bk5"], ["krT_s"])
                for h in range(4):
                    mm(psum[:, 2 * 512 + h * 128:2 * 512 + (h + 1) * 128], vrb[:, h * 128:(h + 1) * 128], qrT[:, h, tsl],
                       True, True, ["krT_s", "qrT%d" % n], ["bk2"])
                tt("dve", kwf[:, :], bank(2), DTf, ALU.mult, ["bk2", "DT0", "DT1", "DT2", "DT3"], ["msk"])
                tt(os.environ.get("K_PA", "pool"), qf_s.rearrange("p (h c) -> p h c", h=4), qrT[:, :, tsl], QDF[:], ALU.mult,
                   ["qrT%d" % n] + ["QDF%d" % h for h in range(4)], ["qf_s"])
                tt(os.environ.get("K_PA", "pool"), qb_s.rearrange("p (h c) -> p h c", h=4), qrT[:, :, tsl], QDB[:], ALU.mult,
                   ["qrT%d" % n] + ["QDB%d" % h for h in range(4)], ["qb_s"])
                for h in range(4):
                    hs = slice(h * 128, (h + 1) * 128)
                    o_ = psum[:, 3 * 512 + h * 128:3 * 512 + (h + 1) * 128]
                    mm(o_, kwf[:, hs], vr[:, n, hs], True, False, ["msk", "vr%d" % n], ["bk3"])
                    mm(o_, qf_s[:, hs], b512[:, hs], False, False, ["qf_s", "b512_lo", "b512_hi"], ["bk3"])
                    mm(o_, qb_s[:, hs], SbN[:, n, hs], False, True, ["qb_s", "st%d" % (n // 4)], ["bk3"])
                cp("act", m1r[:, :], bank(3), ["bk3"], ["m1r"])
                for h in range(4):
                    hs = slice(h * 128, (h + 1) * 128)
                    P.add("dve", lambda e, h=h, hs=hs: e.bn_stats(out=sm[:, 32 + 6 * h:38 + 6 * h], in_=m1r[:, hs]),
                          reads=["m1r"], writes=["bnst%d" % h])
                    P.add("dve", lambda e, h=h: e.bn_aggr(out=sm[:, 56 + 2 * h:58 + 2 * h], in_=sm[:, 32 + 6 * h:38 + 6 * h]),
                          reads=["bnst%d" % h], writes=["mv%d" % h])
                mv3 = sm[:, 56:64].rearrange("p (h c) -> p h c", c=2)
                act(sm[:, 64:68], mv3[:, :, 1], AF.Ln, ["mv%d" % h for h in range(4)], ["rs4"], scale=1.0, bias=EPS)
                act(sm[:, 64:68], sm[:, 64:68], AF.Exp, ["rs4"], ["rs4"], scale=-0.5)
                for h in range(4):
                    hs = slice(h * 128, (h + 1) * 128)
                    ts("dve", m2r[:, hs], m1r[:, hs], sm[:, 56 + 2 * h:57 + 2 * h], sm[:, 64 + h:65 + h], ALU.subtract,
                       ALU.mult, ["m1r", "mv%d" % h, "rs4"], ["on%d" % h])
                tt("pool", mixed[:, n, 512:1024], m2r[:, :], gnw_s[:, :], ALU.mult, ["on%d" % h for h in range(4)] + ["gnw", "WB_done"],
                   ["mixR%d" % n])
                tt("pool", kwb[:, :].rearrange("p (h d) -> p h d", h=4), kr[:, n, :].rearrange("p (h d) -> p h d", h=4),
                   bc(kdec[:, 0:4], 2, 128), ALU.mult, KR(n) + ["kdecf"], ["kwb"])
                for h in range(4):
                    mm(psum[:, 4 * 512 + h * 128:4 * 512 + (h + 1) * 128], kwb[:, h * 128:(h + 1) * 128],
                       vr[:, n, h * 128:(h + 1) * 128], True, True, ["kwb", "vr%d" % n], ["bk4"])
                for h in range(4):
                    stt(Sf0[:, h * 128:(h + 1) * 128], Sf0[:, h * 128:(h + 1) * 128], sm[:, h:h + 1],
                        psum[:, 4 * 512 + h * 128:4 * 512 + (h + 1) * 128], ALU.mult, ALU.add, ["Sf0", "cd8", "bk4"], ["Sf0"])
            dump("mixed", mixed.rearrange("p t c -> p (t c)"), ["mixR%d" % n for n in range(nown)])

        if stop_after not in ("p1", "ret"):
            for t in range(nown):
                i = xproc(t, False)
                tb = "tabs%d" % i
                inproj(i, W_A, 0, 512, 2, "WA")
                act(sqt[:, :], bank(2), AF.Square, ["bk2"], ["sqt"])
                P.add("dve", lambda e: e.tensor_reduce(out=sm[:, 72:80], in_=sqt[:, :].rearrange("p (h d) -> p h d", h=8),
                                                       axis=AX.X, op=ALU.add), reads=["sqt"], writes=["rq"])
                rsqrt_chain(sm[:, 72:80], sm[:, 72:80], 1.0 / 64, ["rq"], "rq")
                tt("dve", m1r[:, :].rearrange("p (h d) -> p h d", h=8), bank(2).rearrange("p (h d) -> p h d", h=8),
                   bc(sm[:, 72:80], 2, 64), ALU.mult, ["bk2", "rq"], ["m1r"])
                tt(os.environ.get("K_PB", "pool"), m1r[:, :].rearrange("p (h d) -> p h d", h=8), m1r[:, :].rearrange("p (h d) -> p h d", h=8),
                   bc(qkn_s[:, 0:64], 1, 8), ALU.mult, ["m1r", "qkn"], ["m1r"])
                rope(m1r[:, :], 8, 64, tabs[:, i, 0:32], tabs[:, i, 32:64], q512[:, :], m2r, krf, ["m1r"], [tb], "qro2", ["on0", "on1", "on2", "on3"], ["qf_s", "qb_s"])
                pb = bankbf(5)
                for pr in range(4):
                    tr(pb[:, 512 + pr * 128:512 + (pr + 1) * 128], q512[:, pr * 128:(pr + 1) * 128],
                       ["qro2_lo", "qro2_hi"], ["bk5"])
                cp("act", qT[:, :, t * 128:(t + 1) * 128], pb[:, 512:1024].rearrange("p (h c) -> p h c", h=4), ["bk5"],
                   ["qrT%d" % t])
                inproj(i, W_A, 512, 512, 3, "WA")
                act(sqt[:, :], bank(3), AF.Exp, ["bk3"], ["sqt"], scale=-1.0)
                act(sqt[:, :], sqt[:, :], AF.Ln, ["sqt"], ["sqt"], scale=1.0, bias=1.0)
                act(sqt[:, :], sqt[:, :], AF.Exp, ["sqt"], ["sqt"], scale=-1.0)
                tt("dve", ga[:, t, :], bank(3), sqt[:, :], ALU.mult, ["bk3", "sqt"], KR(t))
                inproj(i, W_A, 1024, 512, 4, "WA")
                act(Sb0[:, :], bank(4), AF.Exp, ["bk4"], ["Sb0"], scale=-1.0)
                act(Sb0[:, :], Sb0[:, :], AF.Ln, ["Sb0"], ["Sb0"], scale=1.0, bias=1.0)
                act(Sb0[:, :], Sb0[:, :], AF.Exp, ["Sb0"], ["Sb0"], scale=-1.0)
                tt("dve", Sb0[:, :], bank(4), Sb0[:, :], ALU.mult, ["bk4", "Sb0"], ["Sb0"])
                tt("pool", mixed[:, t, 512:1024], mixed[:, t, 512:1024], Sb0[:, :], ALU.mult, ["mixR%d" % t, "Sb0"],
                   ["mixR%d" % t])

        if stop_after not in ("p1", "ret", "p2"):
            P.add("pool", lambda e: e.memset(dummy[:, 2:4], 0.0), reads=[], writes=["WA", "WA_done", "dummy1"], cost=0.1)
            it = 0
            nkt = int(os.environ.get("K_NKT", str(NT)))
            for qb in range(nown // 4):
                for pair in range(4):
                    accb = 4
                    it += 1
                    for kt in range(nkt):
                        si = kt % 2
                        pbuf = kt % NPT
                        for g in range(2):
                            mm(bank(2 * si + g), kT[g * 64:(g + 1) * 64, kt * 128:(kt + 1) * 128],
                               qT[g * 64:(g + 1) * 64, pair, qb * 512:(qb + 1) * 512], True, True,
                               ["kT"] + ["qrT%d" % (qb * 4 + u) for u in range(4)], ["bk%d" % (2 * si + g)])
                        act(PT[:, pbuf, :], psum[:, 2 * si * 512:(2 * si + 2) * 512], AF.Exp,
                            ["bk%d" % (2 * si), "bk%d" % (2 * si + 1)], ["PT%d" % pbuf, "vr%d" % (2 * pbuf), "vr%d" % (2 * pbuf + 1)], scale=0.125)
                        for g in range(2):
                            for sub in range(4):
                                c0 = (accb + g) * 512 + sub * 65
                                mm(psum[:, c0:c0 + 65], PT[:, pbuf, g * 512 + sub * 128:g * 512 + (sub + 1) * 128],
                                   Vaug[:, kt, g, :], kt == 0 and sub == 0, kt == nkt - 1, ["PT%d" % pbuf, "Vaug", "Vones"],
                                   bkres(accb + g), skip=True)
                    for g in range(2):
                        h = pair + 4 * g
                        acc3 = psum[:, (accb + g) * 512:(accb + g) * 512 + 260].rearrange("p (s c) -> p s c", s=4)
                        rr = sm[:, 80 + 4 * g:84 + 4 * g]
                        recip(rr, acc3[:, :, 64], bkres(accb + g), ["rr%d" % g])
                        tmp3 = sqt[:, g * 256:(g + 1) * 256].rearrange("p (s c) -> p s c", s=4)
                        tt("dve", tmp3, acc3[:, :, 0:64], bc(rr, 2, 64), ALU.mult, bkres(accb + g) + ["rr%d" % g],
                           ["atmp%d" % g])
                        tt("pool", mixed[:, qb * 4:(qb + 1) * 4, h * 64:(h + 1) * 64], tmp3,
                           ga[:, qb * 4:(qb + 1) * 4, h * 64:(h + 1) * 64], ALU.mult,
                           ["atmp%d" % g, "WB_done"] + [x for u in range(4) for x in KR(qb * 4 + u)],
                           ["mixA%d_%d" % (qb * 4 + u, h) for u in range(4)])
            dump("mixed2", mixed.rearrange("p t c -> p (t c)"),
                 ["mixA%d_%d" % (t, h) for t in range(nown) for h in range(8)] + ["mixR%d" % t for t in range(nown)])

            dma("sp", g_bc[:], fnw, "fnw", writes=["fnw_s"] + GB)
            for k in range(8):
                s_ = next_slot()
                dma("sp", stage[:, s_, :], w_out[k * 128:(k + 1) * 128, :], "st%d" % s_, writes=["st%d" % s_])
                tt(("dve", "pool")[k % 2], wo[:, k, :], stage[:, s_, :], gate_bc[:], ALU.mult, ["st%d" % s_, "WA_done"] + GATE, ["wo"])
            mixres = lambda t: ["mixA%d_%d" % (t, h) for h in range(8)] + ["mixR%d" % t]
            for t in range(nown):
                mb = t % 4
                tbk, ybk = {nown - 3: (0, 1), nown - 2: (2, 3), nown - 1: (4, 5)}.get(t, (6, 7))
                pb = bankbf(tbk)
                P.slack = {"pe": float(os.environ.get("K_SLACK_PE", "1.5"))}
                for k in range(8):
                    tr(pb[:, k * 128:(k + 1) * 128], mixed[:, t, k * 128:(k + 1) * 128], mixres(t), ["bk%d" % tbk])
                P.slack = {}
                cp("dve", mT[:, mb, :, :].rearrange("p k c -> p (k c)"), pb, ["bk%d" % tbk, "WA_done"], ["mT%d" % mb])
                s_ = next_slot()
                st = "st%d" % s_
                dma("sp", stage[:, s_, :], xo[t * 128:(t + 1) * 128, :], st, writes=[st])
                for c in range(2):
                    for k in range(8):
                        mm(bank(ybk), mT[:, mb, k, :], wo[:, k, c * 512:(c + 1) * 512], k == 0, k == 7,
                           ["mT%d" % mb, "wo"], ["bk%d" % ybk])
                    tt("dve", stage[:, s_, c * 512:(c + 1) * 512], bank(ybk), stage[:, s_, c * 512:(c + 1) * 512], ALU.add,
                       ["bk%d" % ybk, st], [st])
                ssq = sm[:, 100 + mb:101 + mb]
                P.add("dve", lambda e, s_=s_, ssq=ssq: e.scalar_tensor_tensor(
                    out=krf_bf, in0=stage[:, s_, :], scalar=1.0, in1=stage[:, s_, :], op0=ALU.mult, op1=ALU.mult,
                    accum_out=ssq), reads=[st], writes=["fss%d" % mb, "qf_s", "qb_s"], cost=1.4)
                P.slack = {"act": float(os.environ.get("K_SLACK_ACT", "0.5"))}
                rsqrt_chain(ssq, ssq, 1.0 / D, ["fss%d" % mb], "fss%d" % mb)
                P.slack = {}
                stt(stage[:, s_, :], stage[:, s_, :], ssq, g_bc[:], ALU.mult, ALU.mult, [st, "fss%d" % mb, "fnw_s"], [st])
                dma("sp", y[t * 128:(t + 1) * 128, :], stage[:, s_, :], "out%d" % s_, reads=[st], writes=["y%d" % t])

        P.slack = {}
        fin_reads = ["dbg_" + n for n in dbg_out] + ["y%d" % t for t in range(NOWN)]
        P.add("sp", lambda e: e.nop(), reads=fin_reads, name="final")
        P.resolve()
        with nc.Block() as block:
            P.emit(block, sems, dsem)
    return nc


def _rope_tables(n_rows, head_dim):
    rows, cols = np.meshgrid(np.arange(n_rows, dtype=np.float32), np.arange(64, dtype=np.float32), indexing="ij")
    rows = rows.reshape(-1)
    cols = cols.reshape(-1)
    n_axis = head_dim // 4
    inv_freq = (np.float32(10000.0) ** (-np.arange(n_axis, dtype=np.float32) / np.float32(n_axis))).astype(np.float32)
    ang = np.concatenate([rows[:, None] * inv_freq, cols[:, None] * inv_freq], axis=-1).astype(np.float32)
    return np.cos(ang).astype(np.float32), np.sin(ang).astype(np.float32)


def _host_inputs(x, c, ctx, c_ctx, norm_w, w_mod, b_mod, w_in, q_norm_w, k_norm_w, ret_decay_fwd, ret_decay_bwd,
                 ret_gn_w, w_out, final_norm_w):
    f32 = np.float32
    x = np.asarray(x, f32)
    ctx = np.asarray(ctx, f32)
    w_in0 = np.asarray(w_in, f32)[0]
    qa = np.arange(0, 512).reshape(8, 64)
    qa_perm = np.concatenate([np.concatenate([qa[i], qa[i + 4]]) for i in range(4)])
    cols = np.concatenate([qa_perm, np.arange(768, 1280), np.arange(2816, 3328),
                           np.arange(512, 640), np.arange(640, 768), np.arange(1280, 1792),
                           np.arange(1792, 2304), np.arange(2304, 2816)])
    w_in_p = np.ascontiguousarray(w_in0[:, cols])
    cos_a, sin_a = _rope_tables(SEQ // 64, 64)
    cos_r, sin_r = _rope_tables(SEQ // 64, 128)
    ksc = f32(128.0 ** -0.5)
    tab_lat = np.concatenate([cos_a, sin_a, cos_r, sin_r, cos_r * ksc, sin_r * ksc], axis=1).astype(f32)
    tab_ctx = np.zeros((CTX, 320), f32)
    tab_ctx[:, 0:32] = 1.0
    tab_ctx[:, 64:128] = 1.0
    tab_ctx[:, 192:256] = ksc
    p = np.arange(128, dtype=f32)
    ii = np.arange(128, dtype=f32)
    cst = np.zeros((128, 898), f32)
    cst[:, 0] = 127 - p
    cst[:, 1] = p
    cst[:, 2:130] = (ii + 1)[None, :]
    cst[:, 130:258] = (128 - ii)[None, :]
    dif = ii[None, :] - p[:, None]
    cst[:, 258:386] = np.maximum(dif, 0)
    cst[:, 386:514] = np.maximum(-dif, 0)
    cst[:, 514:642] = (dif > 0)
    cst[:, 642:770] = (dif < 0)
    cst[:, 770:898] = 2.0 * (dif == 0)
    sel = np.zeros((2, 256), f32)
    sel[0, 0:128] = 1.0
    sel[1, 128:256] = 1.0
    ident = np.eye(128, dtype=f32)
    common = dict(
        w_mod=np.ascontiguousarray(np.asarray(w_mod, f32)[0]),
        b_mod2=np.ascontiguousarray(np.broadcast_to(np.asarray(b_mod, f32)[0][None, :], (2, 3 * D))),
        norm_w2=np.ascontiguousarray(np.broadcast_to(np.asarray(norm_w, f32)[0][None, :], (2, D))),
        w_in=w_in_p,
        qkn=np.ascontiguousarray(np.broadcast_to(
            np.concatenate([np.asarray(q_norm_w, f32)[0], np.asarray(k_norm_w, f32)[0]])[None, :], (128, 128))),
        dec=np.ascontiguousarray(np.broadcast_to(
            np.concatenate([np.asarray(ret_decay_fwd, f32)[0], np.asarray(ret_decay_bwd, f32)[0]])[None, :], (128, 8))),
        gnw=np.ascontiguousarray(np.broadcast_to(np.asarray(ret_gn_w, f32)[0][None, :], (128, 512))),
        fnw=np.ascontiguousarray(np.broadcast_to(np.asarray(final_norm_w, f32)[None, :], (128, D))),
        w_out=np.ascontiguousarray(np.asarray(w_out, f32)[0]),
        ident=ident, cst=cst, sel=sel,
    )
    in_maps = []
    c = np.asarray(c, f32)
    c_ctx = np.asarray(c_ctx, f32)
    for core in range(8):
        b, s = divmod(core, 4)
        t0, t1 = s * 2048, (s + 1) * 2048
        own = np.arange(t0, t1)
        oth = np.concatenate([np.arange(0, t0), np.arange(t1, SEQ)])
        xo = np.concatenate([x[b, own], x[b, oth], ctx[b]], axis=0)
        tabc = np.concatenate([tab_lat[own], tab_lat[oth], tab_ctx], axis=0).reshape(NT, 128, 320)
        df = np.where(oth < t0, t0 - 1 - oth, BIGD).astype(f32)
        db = np.where(oth >= t1, oth - t1, BIGD).astype(f32)
        cc = np.arange(CTX, dtype=f32)
        df = np.concatenate([df, t0 + 255 - cc])
        db = np.concatenate([db, SEQ + cc - t1])
        dist = np.stack([df.reshape(50, 128).T, db.reshape(50, 128).T], axis=-1).reshape(128, 100).astype(f32)
        cvv = np.stack([c[b].reshape(8, 128).T, c_ctx.reshape(8, 128).T], axis=-1).reshape(128, 16).astype(f32)
        m = dict(common)
        m.update(xo=np.ascontiguousarray(xo), tab=np.ascontiguousarray(tabc), dist=np.ascontiguousarray(dist),
                 cv=np.ascontiguousarray(cvv))
        in_maps.append(m)
    return in_maps


_NC_CACHE = {}


def kernel(x, c, ctx, c_ctx, norm_w, w_mod, b_mod, w_in, q_norm_w, k_norm_w, ret_decay_fwd, ret_decay_bwd,
           ret_gn_w, w_out, final_norm_w):
    in_maps = _host_inputs(x, c, ctx, c_ctx, norm_w, w_mod, b_mod, w_in, q_norm_w, k_norm_w, ret_decay_fwd,
                           ret_decay_bwd, ret_gn_w, w_out, final_norm_w)
    if "nc" not in _NC_CACHE:
        _NC_CACHE["nc"] = build_program()
    nc = _NC_CACHE["nc"]
    res = run_bass_kernel_spmd(nc, in_maps, core_ids=list(range(8)))
    out = np.zeros((2, SEQ, D), np.float32)
    for core in range(8):
        b, s = divmod(core, 4)
        out[b, s * 2048:(s + 1) * 2048] = res.results[core]["y"]
    return out
```
